# Optimizing a Trainium2 kernel written in Bass

```python
import jax, jax.numpy as jnp
from jax import lax
import numpy as np

D_MODEL = 1024
BATCH = 32
SEQ = 2048
DEPTH = 4
DEC_BATCH = 32
DEC_SEQ = 16
PAST_LEN = 4096

CHUNK = 64
MIX_WIDTH = D_MODEL
POOL_WIDTH = MIX_WIDTH // 2
POOL_WINDOWS = (2, 4, 8, 16)
N_POOL_GROUPS = len(POOL_WINDOWS)
POOL_GROUP = POOL_WIDTH // N_POOL_GROUPS
POOL_HIST = max(POOL_WINDOWS) - 1
HEAD_DIM = 64
ATTN_WIDTH = MIX_WIDTH - POOL_WIDTH
N_Q_HEADS = ATTN_WIDTH // HEAD_DIM
N_KV_HEADS = 2
GQA_GROUP = N_Q_HEADS // N_KV_HEADS
KV_WIDTH = N_KV_HEADS * HEAD_DIM
WINDOW = 128
WIN_CHUNKS = WINDOW // CHUNK
IN_WIDTH = POOL_WIDTH + ATTN_WIDTH + 2 * KV_WIDTH
D_FF = -(-8 * D_MODEL // (3 * 256)) * 256
EPS = 1e-6

kernel_name = "hymba_pool_swa_sink_stream_step"


def rmsnorm(x, g):
    xf = x.astype(jnp.float32)
    y = xf * lax.rsqrt(jnp.mean(xf * xf, axis=-1, keepdims=True) + EPS)
    return (y * g.astype(jnp.float32)).astype(x.dtype)


def multiscale_pool(u, hist, pos0, w_lin, scale):
    B, T, P = u.shape
    z = jnp.concatenate([hist, u], axis=1).astype(jnp.float32)
    cs = jnp.concatenate([jnp.zeros((B, 1, P), jnp.float32), jnp.cumsum(z, axis=1)], axis=1)
    pos = pos0 + jnp.arange(T)
    means = []
    for g, w in enumerate(POOL_WINDOWS):
        sl = slice(g * POOL_GROUP, (g + 1) * POOL_GROUP)
        end = cs[:, POOL_HIST + 1:POOL_HIST + 1 + T, sl]
        start = cs[:, POOL_HIST + 1 - w:POOL_HIST + 1 - w + T, sl]
        cnt = jnp.minimum(pos + 1, w).astype(jnp.float32)[None, :, None]
        means.append((end - start) / cnt)
    d = (jnp.concatenate(means, axis=-1) - u.astype(jnp.float32)).astype(u.dtype)
    d = d.reshape(B, T, N_POOL_GROUPS, POOL_GROUP)
    y = jnp.einsum("btgc,gcd->btgd", d, w_lin).reshape(B, T, P)
    return y * scale


def sink_attention(q, k, v, mask, sinks):
    s = jnp.einsum("...qhgd,...khd->...hgqk", q, k).astype(jnp.float32) * (HEAD_DIM ** -0.5)
    s = jnp.where(mask, s, -jnp.inf)
    sink = sinks.astype(jnp.float32)[:, :, None, None]
    m = jnp.maximum(jnp.max(s, axis=-1, keepdims=True), sink)
    p = jnp.exp(s - m)
    denom = jnp.sum(p, axis=-1, keepdims=True) + jnp.exp(sink - m)
    return jnp.einsum("...hgqk,...khd->...qhgd", (p / denom).astype(v.dtype), v)


def banded_window_attention(q, k, v, sinks):
    B, T = q.shape[:2]
    NC = T // CHUNK
    qb = q.reshape(B, NC, CHUNK, N_KV_HEADS, GQA_GROUP, HEAD_DIM)
    pad = ((0, 0), (WIN_CHUNKS * CHUNK, 0), (0, 0), (0, 0))
    kp = jnp.pad(k, pad).reshape(B, NC + WIN_CHUNKS, CHUNK, N_KV_HEADS, HEAD_DIM)
    vp = jnp.pad(v, pad).reshape(B, NC + WIN_CHUNKS, CHUNK, N_KV_HEADS, HEAD_DIM)
    kb = jnp.concatenate([kp[:, j:j + NC] for j in range(WIN_CHUNKS + 1)], axis=2)
    vb = jnp.concatenate([vp[:, j:j + NC] for j in range(WIN_CHUNKS + 1)], axis=2)
    kb_len = (WIN_CHUNKS + 1) * CHUNK
    key_chunk = jnp.arange(NC)[:, None] - WIN_CHUNKS + jnp.arange(kb_len)[None, :] // CHUNK
    mask = (key_chunk >= 0)[None, :, None, None, None, :]
    o = sink_attention(qb, kb, vb, mask, sinks)
    return o.reshape(B, T, ATTN_WIDTH)


def cached_window_attention(q, k_all, v_all, pos0, sinks):
    B, T = q.shape[:2]
    qpos = pos0 + jnp.arange(T)
    kpos = pos0 - WINDOW + jnp.arange(WINDOW + T)
    qc = (qpos // CHUNK)[:, None]
    kc = (kpos // CHUNK)[None, :]
    mask = (kc <= qc) & (kc >= qc - WIN_CHUNKS) & (kpos[None, :] >= 0)
    qg = q.reshape(B, T, N_KV_HEADS, GQA_GROUP, HEAD_DIM)
    o = sink_attention(qg, k_all, v_all, mask, sinks)
    return o.reshape(B, T, ATTN_WIDTH)


def trunk_layer(x, c, hist_pool, hist_k, hist_v, pos0, w_ada, b_ada, g_mix, w_in, pool_w,
                pool_scale, sinks, w_out, g_ffn, w_gate_up, w_down):
    B, T, _ = x.shape
    mod = (jax.nn.silu(c) @ w_ada + b_ada)[:, None, :]
    sh1, sc1, ga1, sh2, sc2, ga2 = jnp.split(mod, 6, axis=-1)
    h = rmsnorm(x, g_mix) * (1 + sc1) + sh1
    proj = h @ w_in
    u, q, k, v = jnp.split(proj, [POOL_WIDTH, POOL_WIDTH + ATTN_WIDTH,
                                  POOL_WIDTH + ATTN_WIDTH + KV_WIDTH], axis=-1)
    q = q.reshape(B, T, N_Q_HEADS, HEAD_DIM)
    k = k.reshape(B, T, N_KV_HEADS, HEAD_DIM)
    v = v.reshape(B, T, N_KV_HEADS, HEAD_DIM)
    pool_out = multiscale_pool(u, hist_pool, pos0, pool_w, pool_scale)
    new_pool = jnp.concatenate([hist_pool, u], axis=1)[:, -POOL_HIST:]
    sk = sinks.reshape(N_KV_HEADS, GQA_GROUP)
    if hist_k is None:
        attn_out = banded_window_attention(q, k, v, sk)
        new_k, new_v = k[:, -WINDOW:], v[:, -WINDOW:]
    else:
        k_all = jnp.concatenate([hist_k, k], axis=1)
        v_all = jnp.concatenate([hist_v, v], axis=1)
        attn_out = cached_window_attention(q, k_all, v_all, pos0, sk)
        new_k, new_v = k_all[:, -WINDOW:], v_all[:, -WINDOW:]
    mix = jnp.concatenate([pool_out, attn_out], axis=-1) @ w_out
    x = x + ga1 * mix
    h = rmsnorm(x, g_ffn) * (1 + sc2) + sh2
    a, b = jnp.split(h @ w_gate_up, 2, axis=-1)
    x = x + ga2 * ((jax.nn.silu(a) * b) @ w_down)
    return x, new_pool, new_k, new_v


def setup_inputs(seed: int = 0) -> dict:
    key = jax.random.key(seed)
    ks = jax.random.split(key, 19)
    f32 = jnp.float32

    def nrm(k, shape, s):
        return jax.random.normal(k, shape, f32) * s

    return {
        "x_prompt": nrm(ks[0], (BATCH, SEQ, D_MODEL), 1.0),
        "x_sample": nrm(ks[1], (DEC_BATCH, DEC_SEQ, D_MODEL), 1.0),
        "cache_pool": nrm(ks[2], (DEPTH, DEC_BATCH, POOL_HIST, POOL_WIDTH), 1.0),
        "cache_k": nrm(ks[3], (DEPTH, DEC_BATCH, WINDOW, N_KV_HEADS, HEAD_DIM), 1.0),
        "cache_v": nrm(ks[4], (DEPTH, DEC_BATCH, WINDOW, N_KV_HEADS, HEAD_DIM), 1.0),
        "c_prompt": nrm(ks[5], (BATCH, D_MODEL), 1.0),
        "c_sample": nrm(ks[6], (DEC_BATCH, D_MODEL), 1.0),
        "w_ada": nrm(ks[7], (DEPTH, D_MODEL, 6 * D_MODEL), 0.5 * D_MODEL ** -0.5),
        "b_ada": nrm(ks[8], (DEPTH, 6 * D_MODEL), 0.02),
        "g_mix": 1.0 + nrm(ks[9], (DEPTH, D_MODEL), 0.02),
        "w_in": nrm(ks[10], (DEPTH, D_MODEL, IN_WIDTH), D_MODEL ** -0.5),
        "pool_w": nrm(ks[11], (DEPTH, N_POOL_GROUPS, POOL_GROUP, POOL_GROUP), POOL_GROUP ** -0.5),
        "pool_scale": 1.0 + nrm(ks[12], (DEPTH, POOL_WIDTH), 0.1),
        "sinks": nrm(ks[13], (DEPTH, N_Q_HEADS), 1.0),
        "w_out": nrm(ks[14], (DEPTH, MIX_WIDTH, D_MODEL), MIX_WIDTH ** -0.5),
        "g_ffn": 1.0 + nrm(ks[15], (DEPTH, D_MODEL), 0.02),
        "w_gate_up": nrm(ks[16], (DEPTH, D_MODEL, 2 * D_FF), D_MODEL ** -0.5),
        "w_down": nrm(ks[17], (DEPTH, D_FF, D_MODEL), D_FF ** -0.5),
        "g_final": 1.0 + nrm(ks[18], (D_MODEL,), 0.02),
    }


def reference(x_prompt, x_sample, cache_pool, cache_k, cache_v, c_prompt, c_sample, w_ada, b_ada,
              g_mix, w_in, pool_w, pool_scale, sinks, w_out, g_ffn, w_gate_up, w_down, g_final):
    xp, xs = x_prompt, x_sample
    zero_hist = jnp.zeros((xp.shape[0], POOL_HIST, POOL_WIDTH), xp.dtype)
    pool_p, k_p, v_p, pool_s, k_s, v_s = [], [], [], [], [], []
    for l in range(DEPTH):
        lw = (w_ada[l], b_ada[l], g_mix[l], w_in[l], pool_w[l], pool_scale[l], sinks[l],
              w_out[l], g_ffn[l], w_gate_up[l], w_down[l])
        xp, npool, nk, nv = trunk_layer(xp, c_prompt, zero_hist, None, None, 0, *lw)
        pool_p.append(npool); k_p.append(nk); v_p.append(nv)
        xs, npool, nk, nv = trunk_layer(xs, c_sample, cache_pool[l], cache_k[l], cache_v[l],
                                        PAST_LEN, *lw)
        pool_s.append(npool); k_s.append(nk); v_s.append(nv)
    y_prompt = rmsnorm(xp, g_final)
    y_sample = rmsnorm(xs, g_final)
    return (y_prompt, y_sample, jnp.stack(pool_p), jnp.stack(k_p), jnp.stack(v_p),
            jnp.stack(pool_s), jnp.stack(k_s), jnp.stack(v_s))
```

```python
import contextlib
import numpy as np
import concourse.bass as bass
import concourse.mybir as mybir
from concourse.bass_utils import run_bass_kernel_spmd

F32 = mybir.dt.float32
BF16 = mybir.dt.bfloat16
AF = mybir.ActivationFunctionType
ALU = mybir.AluOpType

D = 1024
KC = 8
NL = 4
SEQ = 2048
NF = 22
POOLW = (2, 4, 8, 16)
EPS = 1e-6
FGROUPS = ((0, 8), (8, 16), (16, 22))
ENGS = ("pe", "act", "dve", "pool", "sp")


class Sched:
    def __init__(self, nc):
        self.nc = nc
        self.streams = {e: [] for e in ENGS}
        self.count = {e: 0 for e in ENGS}
        self.seen = {e: {} for e in ENGS}
        self.last_w = {}
        self.readers = {}
        self.semkeys = list(ENGS)
        self.nchan = 0

    def new_chan(self):
        k = f"d{self.nchan}"
        self.nchan += 1
        self.semkeys.append(k)
        self.count[k] = 0
        return k

    def _need(self, eng, marker):
        if marker is None:
            return
        semkey, val, src = marker
        if src == eng and eng == "pe":
            return
        if self.seen[eng].get(semkey, 0) >= val:
            return
        self.seen[eng][semkey] = val
        self.streams[eng].append(("w", semkey, val))

    def _deps(self, eng, reads, writes):
        for r in reads:
            self._need(eng, self.last_w.get(r))
        for w in writes:
            self._need(eng, self.last_w.get(w))
            rd = self.readers.get(w)
            if rd:
                for sk, (v, src) in rd.items():
                    self._need(eng, (sk, v, src))

    def _commit(self, marker, reads, writes):
        semkey, val, src = marker
        for r in reads:
            d = self.readers.setdefault(r, {})
            if d.get(semkey, (0, None))[0] < val:
                d[semkey] = (val, src)
        for w in writes:
            self.last_w[w] = marker
            self.readers[w] = {}

    def op(self, eng, fn, reads=(), writes=()):
        self._deps(eng, reads, writes)
        self.count[eng] += 1
        marker = (eng, self.count[eng], eng)
        self.streams[eng].append(("i", fn, eng, 1))
        self._commit(marker, reads, writes)

    def dma(self, queue, chan, fn, reads=(), writes=()):
        self._deps(queue, reads, writes)
        self.count[chan] += 16
        marker = (chan, self.count[chan], "dma")
        self.streams[queue].append(("i", fn, chan, 16))
        self._commit(marker, reads, writes)

    def wait_all(self, eng):
        for k in self.semkeys:
            if self.count[k] > 0 and k != eng:
                self._need(eng, (k, self.count[k], "x"))

    def emit(self, sems):
        nc = self.nc
        streams = self.streams

        def run(engname, handle):
            for ent in streams[engname]:
                if ent[0] == "w":
                    handle.wait_ge(sems[ent[1]], ent[2])
                else:
                    ins = ent[1](handle)
                    ins.then_inc(sems[ent[2]], ent[3])

        with nc.Block() as block:
            @block.tensor
            def _(e):
                run("pe", e)

            @block.scalar
            def _(e):
                run("act", e)

            @block.vector
            def _(e):
                run("dve", e)

            @block.gpsimd
            def _(e):
                run("pool", e)

            @block.sync
            def _(e):
                run("sp", e)


class Ring:
    def __init__(self, items):
        self.items = items
        self.i = 0

    def next(self):
        it = self.items[self.i % len(self.items)]
        self.i += 1
        return it


class TileDesc:
    pass


def make_tiles(n_seq, TT, do_sample):
    tiles = []
    for s in range(n_seq):
        for t0 in range(0, SEQ, TT):
            t = TileDesc()
            t.kind = "p"
            t.seq = s
            t.t0 = t0
            t.ncol = TT
            t.blocks = [(b * 512, 512, [(b * 512, 512, s)]) for b in range(TT // 512)]
            t.first = t0 == 0
            t.last = t0 + TT == SEQ
            t.c0 = t0 // 64
            t.nunits = TT // 64
            t.units_of_block = lambda b: list(range(8 * b, 8 * b + 8))
            tiles.append(t)
    if do_sample:
        t = TileDesc()
        t.kind = "s"
        t.seq = None
        t.t0 = 0
        t.ncol = 64
        t.blocks = [(0, 64, [(16 * i, 16, 4 + i) for i in range(4)])]
        t.first = True
        t.last = True
        t.c0 = 0
        t.nunits = 4
        t.units_of_block = lambda b: [0, 1, 2, 3]
        tiles.append(t)
    return tiles


def piece_plan(tiles, n_layers):
    plan = []
    for ti, t in enumerate(tiles):
        for l in range(n_layers):
            if ti == 0 and l == 0:
                for i in range(12):
                    plan.append(("ada", 0, i))
            reg = [("u", l), ("q", l), ("kv", l), ("pw", l), ("out", l, 0), ("out", l, 1)]
            for gi, (f0, f1) in enumerate(FGROUPS):
                for pi in range((f1 - f0) // 2):
                    reg.append(("gu", l, gi, pi))
                reg += [("dn", l, gi, 0), ("dn", l, gi, 1)]
            if ti == 0 and l + 1 < n_layers:
                out, na = [], 0
                for k, d in enumerate(reg):
                    out.append(d)
                    if k >= 6 and na < 12:
                        out.append(("ada", l + 1, na))
                        na += 1
                reg = out
            plan += reg
    return plan


def build_program(n_seq=4, n_layers=4, TT=1024, do_sample=True, NS=4, dbg_stop=None):
    nc = bass.Bass("TRN2", target_bir_lowering=False)
    S = Sched(nc)

    def din(name, shape):
        return nc.dram_tensor(name, shape, F32, kind="ExternalInput").ap()

    def dout(name, shape):
        return nc.dram_tensor(name, shape, F32, kind="ExternalOutput").ap()

    xp = din("xp", [4, SEQ, D])
    xs = din("xs", [64, D])
    cpool = din("cpool", [NL, 4, 15, 512])
    ck = din("ck", [NL, 4, 128, 128])
    cv = din("cv", [NL, 4, 128, 128])
    call = din("call", [64, 128])
    w_ada = din("w_ada", [NL, D, 6 * D])
    b_ada = din("b_ada", [NL, 48, 128])
    g_mix = din("g_mix", [32, 128])
    w_in = din("w_in", [NL, D, 1280])
    pool_w = din("pool_w", [NL, 4, 128, 128])
    pool_scale = din("pool_scale", [16, 128])
    sinks = din("sinks", [NL, 8])
    w_out = din("w_out", [NL, D, D])
    g_ffn = din("g_ffn", [32, 128])
    w_gu = din("w_gu", [NL, D, 2 * NF * 128])
    w_dn = din("w_dn", [NL, NF * 128, D])
    g_final = din("g_final", [8, 128])

    yp = dout("yp", [4, SEQ, D])
    ys = dout("ys", [64, D])
    poolp = dout("poolp", [NL, 4, 15, 512])
    kp = dout("kp", [NL, 4, 128, 128])
    vp = dout("vp", [NL, 4, 128, 128])
    pools = dout("pools", [NL, 4, 15, 512])
    ks = dout("ks", [NL, 4, 128, 128])
    vs = dout("vs", [NL, 4, 128, 128])

    tiles = make_tiles(n_seq, TT, do_sample)
    main_tiles = [t_ for t_ in tiles if t_.kind == "p"]
    co_tile = None
    if main_tiles and len(main_tiles) < len(tiles):
        co_tile = [t_ for t_ in tiles if t_.kind == "s"][0]
    else:
        main_tiles = tiles
    plan = piece_plan(main_tiles, n_layers)
    reg_ids = {}
    for d in plan:
        if d[0] != "ada" and d not in reg_ids:
            reg_ids[d] = len(reg_ids)
    scr = nc.dram_tensor("wscr", [max(len(reg_ids), 1), 128, 8 * 512], BF16, kind="Internal").ap()
    first_use = {}
    for i, d in enumerate(plan):
        first_use.setdefault(d, i)
    scr_chan = []
    NCH = TT // 64

    with contextlib.ExitStack() as st:
        E = st.enter_context

        def sb(name, shape, dt=F32):
            return E(nc.sbuf_tensor(name, shape, dt))

        X = sb("X", [128, KC, TT])
        R1 = sb("R1", [128, KC, TT], BF16)
        R2 = sb("R2", [128, KC, TT], BF16)
        KT = sb("KT", [128, 128 + TT], BF16)
        NT = max(TT // 128, 4)
        VA = sb("VA", [128, NT, 128], BF16)
        VB = sb("VB", [128, NT, 128], BF16)
        Q2 = sb("Q2", [128, 4, TT], BF16)
        Xs = sb("Xs", [128, KC, 64])
        R1s = sb("R1s", [128, KC, 64], BF16)
        R2s = sb("R2s", [128, KC, 64], BF16)
        KTs = sb("KTs", [128, 320], BF16)
        VAs = sb("VAs", [128, 4, 128], BF16)
        VBs = sb("VBs", [128, 4, 128], BF16)
        Q2s = sb("Q2s", [128, 4, 64], BF16)
        SQs = sb("SQs", [128, KC, 64], BF16)
        WS = [sb(f"WS{i}", [128, 8, 512], BF16) for i in range(NS)]
        SQ = [sb(f"SQ{i}", [128, KC, 512], BF16) for i in range(1)]
        TB = [sb(f"TB{i}", [128, 512]) for i in range(5)]
        RSB = [sb(f"RSB{i}", [128, 512]) for i in range(2)]
        UB = [sb(f"UB{i}", [128, 528]) for i in range(6)]
        XIN = [sb(f"XIN{i}", [128, D]) for i in range(4)]
        GF = sb("GF", [128, D])
        PTF = [sb(f"PTF{i}", [128, 256], BF16) for i in range(5)]
        PTL = [sb(f"PTL{i}", [128, 256], BF16) for i in range(4)]
        PTH = [sb(f"PTH{i}", [128, 256], BF16) for i in range(4)]
        PTS = [sb(f"PTS{i}", [128, 256], BF16) for i in range(4)]
        OA = sb("OA", [128, 128], BF16)
        OB = sb("OB", [128, 128], BF16)
        KVO = [sb(f"KVO{i}", [128, 256]) for i in range(2)]
        POUT = [sb(f"POUT{i}", [16, 512]) for i in range(1)]
        KTC = [sb(f"KTC{i}", [128, 128], BF16) for i in range(4)]
        VCA = [sb(f"VCA{i}", [128, 128], BF16) for i in range(4)]
        VCB = [sb(f"VCB{i}", [128, 128], BF16) for i in range(4)]
        CST = [sb(f"CST{i}", [128, 128]) for i in range(1)]
        CPS = [sb(f"CPS{i}", [16, 512]) for i in range(1)]
        CPT = [sb(f"CPT{i}", [128, 4, 16]) for i in range(4)]
        MODC = sb("MODC", [128, NL, 6, KC, 8])
        GSC = sb("GSC", [128, NL, 2, 8, KC])
        IDENT = sb("IDENT", [128, 128])
        ONES = sb("ONES", [128, 128], BF16)
        ONESF = sb("ONESF", [128, 128])
        BADA = sb("BADA", [128, NL, 48])
        GM = sb("GM", [128, 2, 32])
        PSC = sb("PSC", [128, 16])
        GFC = sb("GFC", [128, 8])
        SCT = sb("SCT", [128, 8, 8], BF16)
        SKX = sb("SKX", [128, NL, 4])
        UH = sb("UH", [128, NL, 4, 16])
        KTH = sb("KTH", [128, NL, 128], BF16)
        VAH = sb("VAH", [128, NL, 128], BF16)
        VBH = sb("VBH", [128, NL, 128], BF16)
        INVC = sb("INVC", [128, 4, 16])
        EPSC = sb("EPSC", [128, 1])
        STG = sb("STG", [128, 128])
        SS = sb("SS", [128, 4])

        for t_ in tiles:
            if t_.kind == "p":
                t_.X, t_.R1, t_.R2, t_.KT, t_.VA, t_.VB, t_.Q2, t_.kp = X, R1, R2, KT, VA, VB, Q2, "m"
                t_.SQ = SQ[0]
            else:
                t_.X, t_.R1, t_.R2, t_.KT, t_.VA, t_.VB, t_.Q2, t_.kp = Xs, R1s, R2s, KTs, VAs, VBs, Q2s, "s"
                t_.SQ = SQs
        PB = [E(nc.psum_tensor(f"PB{i}", [128, 512], F32)) for i in range(8)]
        main_ps = Ring([(PB[i], ("PS", i)) for i in range(3)])
        s_ps = Ring([(PB[3 + i], ("PS", 3 + i)) for i in range(3)])
        od_ps = Ring([(PB[6 + i], ("PS", 6 + i)) for i in range(2)])
        ffn_ps = Ring([(PB[i], ("PS", i)) for i in range(3, 8)])
        all_ps = Ring([(PB[i], ("PS", i)) for i in range(8)])

        tb = Ring([(TB[i], ("TB", i)) for i in range(5)])
        rsb = Ring([(RSB[i], ("RSB", i)) for i in range(2)])
        ub = Ring([(UB[i], ("UB", i)) for i in range(6)])
        xin_free = [(XIN[i], ("XIN", i), S.new_chan()) for i in range(4)]
        pt = {"full": Ring([(PTF[i], ("PTF", i)) for i in range(5)]),
              "lo": Ring([(PTL[i], ("PTL", i)) for i in range(4)]),
              "hi": Ring([(PTH[i], ("PTH", i)) for i in range(4)]),
              "s16": Ring([(PTS[i], ("PTS", i)) for i in range(4)])}
        MROWS = {"full": (0, 128), "lo": (0, 64), "hi": (64, 128), "s16": (0, 16)}
        kvo = Ring([(KVO[i], ("KVO", i), S.new_chan()) for i in range(2)])
        pout = Ring([(POUT[i], ("POUT", i), S.new_chan()) for i in range(1)])
        cst = Ring([(CST[i], ("CST", i), S.new_chan()) for i in range(1)])
        cps = Ring([(CPS[i], ("CPS", i), S.new_chan()) for i in range(1)])
        cache_ring = Ring([(KTC[i], (VCA[i], VCB[i]), CPT[i], ("KTC", i), ("VC", i), ("CPT", i)) for i in range(4)])
        ws_chan = [S.new_chan() for _ in range(NS)]
        d2d_chan = S.new_chan()

        def pe_group(out, pairs, reads, writes):
            def fn(e):
                n = len(pairs)
                ins = None
                for i, (l, r) in enumerate(pairs):
                    ins = e.matmul(out, lhsT=l, rhs=r, start=(i == 0), stop=(i == n - 1))
                return ins
            S.op("pe", fn, reads, writes)

        def pe_T(out, in_, reads, writes):
            S.op("pe", lambda e: e.transpose(out, in_, IDENT[0:in_.shape[0], 0:in_.shape[0]]),
                 list(reads) + ["IDENT"], writes)

        def act(out, in_, func, reads, writes, **kw):
            S.op("act", lambda e: e.activation(out=out, in_=in_, func=func, **kw), reads, writes)

        def dve_tt(out, in0, in1, op, reads, writes):
            S.op("dve", lambda e: e.tensor_tensor(out=out, in0=in0, in1=in1, op=op), reads, writes)

        def dve_stt(out, in0, scalar, in1, op0, op1, reads, writes):
            S.op("dve", lambda e: e.scalar_tensor_tensor(out=out, in0=in0, scalar=scalar, in1=in1,
                                                        op0=op0, op1=op1), reads, writes)

        def dve_ts(out, in0, s1, s2, op0, op1, reads, writes):
            S.op("dve", lambda e: e.tensor_scalar(out=out, in0=in0, scalar1=s1, scalar2=s2, op0=op0, op1=op1),
                 reads, writes)

        def dve_copy(out, in_, reads, writes):
            S.op("dve", lambda e: e.tensor_copy(out=out, in_=in_), reads, writes)

        def dve_recip(out, in_, reads, writes):
            S.op("dve", lambda e: e.reciprocal(out=out, in_=in_), reads, writes)

        def dve_memset(out, val, writes):
            S.op("dve", lambda e: e.memset(out, val), (), writes)

        def sp_dma(chan, out, in_, reads, writes):
            S.dma("sp", chan, lambda e: e.dma_start(out=out, in_=in_), reads, writes)

        class WStream:
            def __init__(self):
                self.next_issue = 0
                self.next_use = 0
                self.free = list(range(NS))
                self.slot_of = {}
                self.held = []

            def _issue(self, i):
                desc = plan[i]
                slot = self.free.pop(0)
                self.slot_of[i] = slot
                W = WS[slot]
                key = ("W", slot)
                ch = ws_chan[slot]
                kind, l = desc[0], desc[1]
                if kind != "ada" and first_use[desc] != i:
                    rid = reg_ids[desc]
                    S.dma("pool", ch, lambda e: e.dma_start(out=W[:, :, :].rearrange("p k c -> p (k c)"),
                                                            in_=scr[rid]), [("SCR", rid)], [key])
                    return

                def dm(out, in_):
                    S.dma("pool", ch, lambda e: e.dma_start(out=out, in_=in_), (), [key])

                if kind == "ada":
                    i0 = desc[2] * 512
                    dm(W[:, :, :], w_ada[l].rearrange("(k p) c -> p k c", p=128)[:, :, i0:i0 + 512])
                elif kind == "u":
                    dm(W[:, :, :], w_in[l].rearrange("(k p) c -> p k c", p=128)[:, :, 0:512])
                elif kind == "q":
                    src = w_in[l].rearrange("(k p) c -> p k c", p=128)
                    for b in range(4):
                        for half in range(2):
                            h = half * 4 + b
                            dm(W[:, :, b * 128 + half * 64: b * 128 + half * 64 + 64],
                               src[:, :, 512 + 64 * h: 512 + 64 * h + 64])
                elif kind == "kv":
                    dm(W[:, :, 0:256], w_in[l].rearrange("(k p) c -> p k c", p=128)[:, :, 1024:1280])
                elif kind == "pw":
                    dm(W[:, 0:4, 0:128], pool_w[l].rearrange("g c d -> c g d"))
                elif kind == "out":
                    c0 = desc[2] * 512
                    dm(W[:, 0:4, :], w_out[l, 0:512, :].rearrange("(k p) c -> p k c", p=128)[:, :, c0:c0 + 512])
                    dm(W[0:64, 4:8, :], w_out[l, 512:768, :].rearrange("(b p) c -> p b c", p=64)[:, :, c0:c0 + 512])
                    dm(W[64:128, 4:8, :], w_out[l, 768:1024, :].rearrange("(b p) c -> p b c", p=64)[:, :, c0:c0 + 512])
                elif kind == "gu":
                    gi, pi = desc[2], desc[3]
                    f = FGROUPS[gi][0] + 2 * pi
                    src = w_gu[l].rearrange("(k p) c -> p k c", p=128)
                    dm(W[:, :, 0:256], src[:, :, 128 * f: 128 * f + 256])
                    dm(W[:, :, 256:512], src[:, :, NF * 128 + 128 * f: NF * 128 + 128 * f + 256])
                elif kind == "dn":
                    gi, h = desc[2], desc[3]
                    f0, f1 = FGROUPS[gi]
                    dm(W[:, 0:f1 - f0, :],
                       w_dn[l, 128 * f0:128 * f1, :].rearrange("(k p) c -> p k c", p=128)[:, :, h * 512:h * 512 + 512])
                else:
                    raise AssertionError(kind)
                if kind != "ada" and len(main_tiles) > 1:
                    rid = reg_ids[desc]
                    if not scr_chan:
                        scr_chan.extend(S.new_chan() for _ in range(NS))
                    S.dma("sp", scr_chan[slot], lambda e: e.dma_start(out=scr[rid],
                                                                       in_=W[:, :, :].rearrange("p k c -> p (k c)")),
                          [key], [("SCR", rid)])

            def _pump(self):
                while self.next_issue < len(plan) and self.free:
                    self._issue(self.next_issue)
                    self.next_issue += 1

            def consume_ada(self):
                self._pump()
                while self.next_use < len(plan) and plan[self.next_use][0] == "ada":
                    i = self.next_use
                    self.next_use += 1
                    assert i < self.next_issue
                    sl = self.slot_of[i]
                    ada_step(plan[i][1], plan[i][2], WS[sl], ("W", sl))
                    self._mark(i)

            def get(self, desc):
                self.consume_ada()
                i = self.next_use
                assert plan[i] == desc, (plan[i], desc)
                assert i < self.next_issue, (i, self.next_issue, desc)
                self.next_use += 1
                self.held.append(i)
                sl = self.slot_of[i]
                return WS[sl], ("W", sl)

            def _mark(self, i):
                self.free.append(self.slot_of.pop(i))
                self._pump()

            def release(self):
                self._mark(self.held.pop(0))

        wq = WStream()

        S.op("pool", lambda e: e.memset(ONESF[:], 1.0), (), ["ONESF"])
        S.op("pool", lambda e: e.affine_select(out=IDENT[:], in_=ONESF[:], pattern=[[1, 128]],
                                               compare_op=ALU.is_equal, fill=0.0, base=0,
                                               channel_multiplier=-1), ["ONESF"], ["IDENT"])
        dve_memset(ONES[:], 1.0, ["ONES"])
        dve_memset(OA[:], 0.0, ["OA"])
        dve_memset(OA[:, 0:64], 1.0, ["OA"])
        dve_memset(OB[:], 0.0, ["OB"])
        dve_memset(OB[:, 64:128], 1.0, ["OB"])
        dve_memset(VA[:], 0.0, [("mVT", i) for i in range(NT)])
        dve_memset(VB[:], 0.0, [("mVT", i) for i in range(NT)])
        dve_memset(Q2[:], 0.0, [("mQ2", b_) for b_ in range(max(TT // 512, 1))])
        dve_memset(VAs[:], 0.0, [("sVT", i) for i in range(4)])
        dve_memset(VBs[:], 0.0, [("sVT", i) for i in range(4)])
        dve_memset(Q2s[:], 0.0, [("sQ2", 0)])
        dve_memset(KTs[:], 0.0, [("sKT", 0)])
        for i in range(4):
            dve_memset(VCA[i][:], 0.0, [("VC", i)])
            dve_memset(VCB[i][:], 0.0, [("VC", i)])
            dve_memset(PTL[i][:], 0.0, [("PTL", i)])
            dve_memset(PTH[i][:], 0.0, [("PTH", i)])
            dve_memset(PTS[i][:], 0.0, [("PTS", i)])
        dve_memset(EPSC[:], EPS, ["EPSC"])
        for g, w in enumerate(POOLW):
            dve_memset(INVC[:, g, :], 1.0 / w, ["INVC"])
            for t in range(w - 1):
                dve_memset(INVC[:, g, t:t + 1], 1.0 / (t + 1), ["INVC"])

        def load_T(src_rows_ap, nrows, dst_ap, dst_keys, func=None):
            ch = S.new_chan()
            sp_dma(ch, STG[0:nrows, :], src_rows_ap, (), ["STG"])
            ps, pk = main_ps.next()
            pe_T(ps[:, 0:nrows], STG[0:nrows, :], ["STG"], [pk])
            if func is None:
                dve_copy(dst_ap, ps[:, 0:nrows], [pk], dst_keys)
            else:
                act(dst_ap, ps[:, 0:nrows], func, [pk], dst_keys)

        for l in range(n_layers):
            load_T(b_ada[l], 48, BADA[:, l, :], ["BADA"])
        load_T(g_mix, 32, GM[:, 0, :], ["GM"])
        load_T(g_ffn, 32, GM[:, 1, :], ["GM"])
        load_T(pool_scale, 16, PSC[:, :], ["PSC"])
        load_T(g_final, 8, GFC[:, :], ["GFC"])
        load_T(call, 64, SCT[:].rearrange("p s k -> p (s k)"), ["SCT"], func=AF.Silu)
        ch = S.new_chan()
        sp_dma(ch, SKX[0:64, :, :], sinks[:, 0:4].unsqueeze(0).to_broadcast([64, NL, 4]), (), ["SKX"])
        ch = S.new_chan()
        sp_dma(ch, SKX[64:128, :, :], sinks[:, 4:8].unsqueeze(0).to_broadcast([64, NL, 4]), (), ["SKX"])
        act(SKX[:], SKX[:], AF.Exp, ["SKX"], ["SKX"])
        ch = S.new_chan()
        sp_dma(ch, GF[:], g_final.rearrange("a b -> (a b)").partition_broadcast(128), (), ["GF"])

        def r1k(kcs, t, blk):
            ks_ = []
            for kc in kcs:
                if kc < 4:
                    ks_.append((t.kp + "R1", kc, "b", blk))
                else:
                    ks_ += [(t.kp + "R1", kc, u) for u in t.units_of_block(blk)]
            return ks_

        def r2k(t, kcs, blk):
            return [(t.kp + "R2", kc, blk) for kc in kcs]

        def xk(t, kcs, blk):
            return [(t.kp + "X", kc, blk) for kc in kcs]

        ALLK = list(range(KC))

        def ada_step(l, i, W, wk):
            ps, pk = main_ps.next()
            for c4 in range(4):
                pe_group(ps[:, c4 * 8:c4 * 8 + 8],
                         [(W[:, k, c4 * 128:(c4 + 1) * 128], SCT[:, :, k]) for k in range(KC)],
                         [wk, "SCT"], [pk])
            for c4 in range(4):
                fc = i * 4 + c4
                j, kc = fc // 8, fc % 8
                act(MODC[:, l, j, kc, :], ps[:, c4 * 8:c4 * 8 + 8], AF.Identity, [pk, "BADA"], [("MODC", l)],
                    bias=BADA[:, l, fc:fc + 1])
            if i == 11:
                for w in range(2):
                    for kc in range(KC):
                        dve_ts(GSC[:, l, w, :, kc], MODC[:, l, 1 + 3 * w, kc, :], 1.0,
                               GM[:, w, l * 8 + kc:l * 8 + kc + 1],
                               ALU.add, ALU.mult, [("MODC", l), "GM"], [("GSC", l)])

        def ld_dma(t, tt):
            buf, bk, ch = ent = xin_free.pop(0)
            if t.kind == "p":
                sp_dma(ch, buf[:, :], xp[t.seq, t.t0 + tt * 128: t.t0 + tt * 128 + 128, :], (), [bk])
            else:
                sp_dma(ch, buf[0:64, :], xs[:, :], (), [bk])
            t.ld[tt] = ent

        def ld_tr(t, tt):
            ent = t.ld.pop(tt)
            buf, bk, _ = ent
            xin_free.append(ent)
            rows = 128 if t.kind == "p" else 64
            blk = (tt * 128) // 512
            for hf in range(2):
                ps, pk = all_ps.next()
                for q4 in range(4):
                    kc = hf * 4 + q4
                    pe_T(ps[:, q4 * 128: q4 * 128 + rows], buf[0:rows, kc * 128:(kc + 1) * 128], [bk], [pk])
                dve_copy(t.X[:, hf * 4:hf * 4 + 4, tt * 128: tt * 128 + rows],
                         ps[:, :].rearrange("p (q c) -> p q c", q=4)[:, :, 0:rows],
                         [pk], xk(t, range(hf * 4, hf * 4 + 4), blk))

        def prologue(t, prefetched):
            if t.kind == "p" and t.first:
                dve_memset(UH[:], 0.0, [("UH", l_) for l_ in range(NL)])
            ntt = t.ncol // 128 if t.kind == "p" else 1
            nb = len(t.blocks)
            issued = prefetched
            done = 0
            for bi in range(nb):
                hi = ntt if bi == nb - 1 else (bi + 1) * 4
                while done < hi:
                    while issued < min(ntt, done + 3) and xin_free:
                        ld_dma(t, issued)
                        issued += 1
                    ld_tr(t, done)
                    done += 1
                norm_A(t, 0, 0, bi)
                if bi < nb - 1:
                    norm_B(t, 0, 0, bi)
            return [lambda: norm_B(t, 0, 0, nb - 1)]

        def norm_A(t, l, w, bi):
            c0, n, subs = t.blocks[bi]
            sq = t.SQ
            act(sq[:, 0:4, 0:n], t.X[:, 0:4, c0:c0 + n], AF.Square, xk(t, range(4), bi), [(t.kp + "SQ", 0)])
            dve_tt(sq[:, 4:8, 0:n], t.X[:, 4:8, c0:c0 + n], t.X[:, 4:8, c0:c0 + n], ALU.mult, xk(t, range(4, 8), bi), [(t.kp + "SQ", 1)])

        def norm_B(t, l, w, bi):
            dst = t.R1 if w == 0 else t.R2
            dstk = (lambda kcs: r1k(kcs, t, bi)) if w == 0 else (lambda kcs: r2k(t, kcs, bi))
            c0, n, subs = t.blocks[bi]
            sq = t.SQ
            ps, pk = main_ps.next()
            pe_group(ps[:, 0:n], [(ONES[:, :], sq[:, k, 0:n]) for k in range(KC)], [(t.kp + "SQ", 0), (t.kp + "SQ", 1), "ONES"], [pk])
            rs, rk = rsb.next()
            act(rs[:, 0:n], ps[:, 0:n], AF.Ln, [pk, "EPSC"], [rk], scale=1.0 / D, bias=EPSC[:, 0:1])
            act(rs[:, 0:n], rs[:, 0:n], AF.Exp, [rk], [rk], scale=-0.5)
            for (sc0, sn, si) in subs:
                o = sc0 - c0
                for kc in range(KC):
                    tmp, tk = tb.next()
                    dve_stt(tmp[:, 0:sn], t.X[:, kc, sc0:sc0 + sn], GSC[:, l, w, si, kc:kc + 1], rs[:, o:o + sn],
                            ALU.mult, ALU.mult, xk(t, [kc], bi) + [rk, ("GSC", l)], [tk])
                    act(dst[:, kc, sc0:sc0 + sn], tmp[:, 0:sn], AF.Identity, [tk, ("MODC", l)], dstk([kc]),
                        bias=MODC[:, l, 3 * w, kc, si:si + 1])

        def evac_gated(t, l, w, kc, bi, ps, pk):
            c0, n, subs = t.blocks[bi]
            for (sc0, sn, si) in subs:
                o = sc0 - c0
                dve_stt(t.X[:, kc, sc0:sc0 + sn], ps[:, o:o + sn], MODC[:, l, 2 + 3 * w, kc, si:si + 1],
                        t.X[:, kc, sc0:sc0 + sn], ALU.mult, ALU.add, [pk, ("MODC", l)] + xk(t, [kc], bi), xk(t, [kc], bi))

        def vt_block(t, l, W, wk, bi):
            H = t.R1
            ps, pk = main_ps.next()
            for q in range(4):
                ti_ = 4 * bi + q
                pe_group(ps[:, q * 128:(q + 1) * 128],
                         [(H[:, k, 128 * ti_:128 * ti_ + 128], W[:, k, 128:256]) for k in range(KC)],
                         [wk] + r1k(ALLK, t, bi), [pk])
            pv = ps[:, :].rearrange("p (q c) -> p q c", q=4)
            vkeys = [(t.kp + "VT", 4 * bi + q) for q in range(4)]
            dve_copy(t.VA[:, 4 * bi:4 * bi + 4, 0:64], pv[:, :, 0:64], [pk], vkeys)
            dve_copy(t.VB[:, 4 * bi:4 * bi + 4, 64:128], pv[:, :, 64:128], [pk], vkeys)

        def win_begin(t, l, pending):
            if len(t.blocks) == 1:
                while pending:
                    pending.pop(0)()
            return (wq.get(("u", l)), wq.get(("q", l)), wq.get(("kv", l)))

        def win_block(t, l, caches, pending, WW, bi):
            H = t.R1
            hk = lambda bi: r1k(ALLK, t, bi)
            (Wu, wuk), (Wq, wqk), (Wkv, wkk) = WW
            bc0, bn, _ = t.blocks[bi]
            if True:
                if t.kind == "p":
                    units = [(bc0, bn, None)]
                else:
                    units = [(16 * i, 16, i) for i in range(4)]
                for g, wdw in enumerate(POOLW):
                    for (c0, n, si) in units:
                        ps, pk = main_ps.next()
                        pe_group(ps[:, 0:n], [(Wu[:, k, g * 128:(g + 1) * 128], H[:, k, c0:c0 + n]) for k in range(KC)],
                                 [wuk] + hk(bi), [pk])
                        U, uk = ub.next()
                        act(U[:, 16:16 + n], ps[:, 0:n], AF.Copy, [pk], [uk])
                        if t.kind == "p":
                            dve_copy(U[:, 1:16], UH[:, l, g, 1:16], [("UH", l)], [uk])
                        else:
                            dve_copy(U[:, 1:16], caches[si][2][:, g, 1:16], [caches[si][5]], [uk])
                        prev, prevk = U, uk
                        lo = 1
                        for lv in range(g + 1):
                            sh = 1 << lv
                            lo2 = lo + sh
                            Sn, sk = ub.next()
                            dve_tt(Sn[:, lo2:16 + n], prev[:, lo2:16 + n], prev[:, lo2 - sh:16 + n - sh], ALU.add,
                                   [prevk], [sk])
                            prev, prevk, lo = Sn, sk, lo2
                        dkey = r2k(t, [4 + g], bi)
                        dve_stt(t.R2[:, 4 + g, c0:c0 + n], prev[:, 16:16 + n], 1.0 / wdw, U[:, 16:16 + n],
                                ALU.mult, ALU.subtract, [prevk, uk], dkey)
                        if t.kind == "p" and t.first and bi == 0:
                            tmp, tk = tb.next()
                            dve_tt(tmp[:, 0:16], prev[:, 16:32], INVC[:, g, :], ALU.mult, [prevk, "INVC"], [tk])
                            dve_tt(t.R2[:, 4 + g, c0:c0 + 16], tmp[:, 0:16], U[:, 16:32], ALU.subtract, [tk, uk], dkey)
                        if t.kind == "p":
                            dve_copy(UH[:, l, g, 1:16], U[:, n + 1:n + 16], [uk], [("UH", l)])
                        yield
                act(t.R2[64:128, 0:4, bc0:bc0 + bn], t.X[64:128, 0:4, bc0:bc0 + bn], AF.Copy, xk(t, range(4), bi),
                    r2k(t, range(4), bi), scale=0.0)
                for oc in range(4):
                    ps, pk = main_ps.next()
                    pe_group(ps[:, 0:bn], [(Wq[:, k, oc * 128:(oc + 1) * 128], H[:, k, bc0:bc0 + bn]) for k in range(KC)],
                             [wqk] + hk(bi), [pk])
                    act(t.R2[0:64, oc, bc0:bc0 + bn], ps[0:64, 0:bn], AF.Copy, [pk], r2k(t, [oc], bi))
                    act(t.Q2[64:128, oc, bc0:bc0 + bn], ps[64:128, 0:bn], AF.Copy, [pk], [(t.kp + "Q2", bi)])
                    yield
                ps, pk = main_ps.next()
                pe_group(ps[:, 0:bn], [(Wkv[:, k, 0:128], H[:, k, bc0:bc0 + bn]) for k in range(KC)], [wkk] + hk(bi), [pk])
                act(t.KT[:, 128 + bc0:128 + bc0 + bn], ps[:, 0:bn], AF.Copy, [pk], [(t.kp + "KT", bi)])
                yield
                while pending:
                    pending.pop(0)()
                if t.kind == "p":
                    vt_block(t, l, Wkv, wkk, bi)
                yield

        def win_end(t, l, WW):
            H = t.R1
            hk = lambda bi: r1k(ALLK, t, bi)
            (Wu, wuk), (Wq, wqk), (Wkv, wkk) = WW
            nb = len(t.blocks)
            if t.kind == "p" and t.last:
                ps, pk = main_ps.next()
                pe_group(ps[0:16, :], [(H[:, k, t.ncol - 16:t.ncol], Wu[:, k, :]) for k in range(KC)],
                         [wuk] + hk(nb - 1), [pk])
                po, pok, pch = pout.next()
                dve_copy(po[0:16, :], ps[0:16, :], [pk], [pok])
                sp_dma(pch, poolp[l, t.seq, :, :], po[1:16, :], [pok], [])
                ps, pk = main_ps.next()
                pe_group(ps[:, 0:256], [(H[:, k, t.ncol - 128:t.ncol], Wkv[:, k, 0:256]) for k in range(KC)],
                         [wkk] + hk(nb - 1), [pk])
                ko, kok, kch = kvo.next()
                dve_copy(ko[:, :], ps[:, 0:256], [pk], [kok])
                sp_dma(kch, kp[l, t.seq, :, :], ko[:, 0:128], [kok], [])
                sp_dma(kch, vp[l, t.seq, :, :], ko[:, 128:256], [kok], [])
            if t.kind == "s":
                for i in range(4):
                    ps, pk = main_ps.next()
                    pe_group(ps[0:16, :], [(H[:, k, 16 * i:16 * i + 16], Wu[:, k, :]) for k in range(KC)],
                             [wuk] + hk(0), [pk])
                    po, pok, pch = pout.next()
                    dve_copy(po[0:16, :], ps[0:16, :], [pk], [pok])
                    sp_dma(pch, pools[l, i, :, :], po[1:16, :], [pok], [])
                    ps, pk = main_ps.next()
                    pe_group(ps[0:16, 0:256], [(H[:, k, 16 * i:16 * i + 16], Wkv[:, k, 0:256]) for k in range(KC)],
                             [wkk] + hk(0), [pk])
                    ko, kok, kch = kvo.next()
                    dve_copy(ko[0:16, :], ps[0:16, 0:256], [pk], [kok])
                    dve_copy(t.VA[0:16, i, 0:64], ps[0:16, 128:192], [pk], [(t.kp + "VT", i)])
                    dve_copy(t.VB[0:16, i, 64:128], ps[0:16, 192:256], [pk], [(t.kp + "VT", i)])
                    sp_dma(kch, ks[l, i, 112:128, :], ko[0:16, 0:128], [kok], [])
                    sp_dma(kch, vs[l, i, 112:128, :], ko[0:16, 128:256], [kok], [])
                    sp_dma(d2d_chan, ks[l, i, 0:112, :], ck[l, i, 16:128, :], (), [])
                    sp_dma(d2d_chan, vs[l, i, 0:112, :], cv[l, i, 16:128, :], (), [])

        def pool_block(t, l, PW, bi):
            W, wk = PW
            c0, n, _ = t.blocks[bi]
            for g in range(4):
                ps, pk = main_ps.next()
                pe_group(ps[:, 0:n], [(W[:, g, 0:128], t.R2[:, 4 + g, c0:c0 + n])], [wk] + r2k(t, [4 + g], bi), [pk])
                act(t.R1[:, g, c0:c0 + n], ps[:, 0:n], AF.Identity, [pk, "PSC"], r1k([g], t, bi),
                    scale=PSC[:, l * 4 + g:l * 4 + g + 1])
                yield

        def attn_S(t, l, u, j, segs, qc0, nq):
            N = 4 * nq
            qblk = qc0 // 512
            Qj = t.R2 if j == 0 else t.Q2
            qkeys = r2k(t, range(4), qblk) if j == 0 else [(t.kp + "Q2", qblk)]
            pts = []
            for si_, (kT, va, vb, mask, rkeys) in enumerate(segs):
                if si_ % 2 == 0:
                    pss, sk = s_ps.next()
                hf = si_ % 2
                pe_group(pss[:, hf * 256:hf * 256 + N], [(kT, Qj[:, 0:4, qc0:qc0 + nq])], list(rkeys) + qkeys, [sk])
                pts.append([None, None, pss, hf, sk, mask])
            for ent in pts:
                pss, hf, sk, mask = ent[2], ent[3], ent[4], ent[5]
                p, pk_ = pt[mask].next()
                r0, r1 = MROWS[mask]
                act(p[r0:r1, 0:N], pss[r0:r1, hf * 256:hf * 256 + N], AF.Exp, [sk], [pk_], scale=0.125)
                ent[0], ent[1] = p, pk_
            return [(e[0], e[1]) for e in pts]

        def attn_PV(t, l, u, segs, pts0, pts1, qc0, nq):
            N = 4 * nq
            ob, obk = od_ps.next()
            pairs_o, pairs_d, rk = [], [], []
            for (seg, (p0, k0), (p1, k1)) in zip(segs, pts0, pts1):
                kT, va, vb, mask, rkeys = seg
                pairs_o += [(va, p0[:, 0:N]), (vb, p1[:, 0:N])]
                pairs_d += [(OA[:, :], p0[:, 0:N]), (OB[:, :], p1[:, 0:N])]
                rk += [k0, k1] + list(rkeys)
            pe_group(ob[:, 0:N], pairs_o, rk, [obk])
            pe_group(ob[:, 256:256 + N], pairs_d, rk + ["OA", "OB"], [obk])
            den, dkk = tb.next()
            dve_tt(den[:, 0:N].rearrange("p (b q) -> p b q", b=4),
                   ob[:, 256:256 + N].rearrange("p (b q) -> p b q", b=4),
                   SKX[:, l, :].unsqueeze(2).to_broadcast([128, 4, nq]), ALU.add,
                   [obk, "SKX"], [dkk])
            act(den[:, 0:N], den[:, 0:N], AF.Ln, [dkk], [dkk])
            act(den[:, 0:N], den[:, 0:N], AF.Exp, [dkk], [dkk], scale=-1.0)
            dve_tt(t.R1[:, 4:8, qc0:qc0 + nq], ob[:, 0:N].rearrange("p (b q) -> p b q", b=4),
                   den[:, 0:N].rearrange("p (b q) -> p b q", b=4), ALU.mult,
                   [obk, dkk], [(t.kp + "R1", 4 + b, u) for b in range(4)])

        def attn_gen(t, l, caches, ulo, uhi):
            units = []

            def tile_seg(idx, mask):
                return (t.KT[:, 128 + 128 * idx:128 + 128 * idx + 128], t.VA[:, idx, :], t.VB[:, idx, :], mask,
                        [(t.kp + "KT", (128 * idx) // 512), (t.kp + "VT", idx)])

            def hist_seg(mask):
                return (KTH[:, l, :], VAH[:, l, :], VBH[:, l, :], mask, [("KTH", l), ("VH", l)])

            if t.kind == "p":
                for c in range(t.nunits):
                    segs = []
                    if c % 2 == 0:
                        if c >= 2:
                            segs.append(tile_seg(c // 2 - 1, "full"))
                        elif not t.first:
                            segs.append(hist_seg("full"))
                        segs.append(tile_seg(c // 2, "lo"))
                    else:
                        if c >= 3:
                            segs.append(tile_seg((c - 3) // 2, "hi"))
                        elif not t.first:
                            segs.append(hist_seg("hi"))
                        segs.append(tile_seg((c - 1) // 2, "full"))
                    units.append((c, segs, 64 * c, 64))
            else:
                for i in range(4):
                    ktc, (vca, vcb), _, kk, vk, _ = caches[i]
                    segs = [(ktc[:, :], vca[:, :], vcb[:, :], "full", [kk, vk]),
                            (t.KT[:, 128 + 16 * i:128 + 16 * i + 128], t.VA[:, i, :], t.VB[:, i, :], "s16",
                             [(t.kp + "KT", 0), (t.kp + "VT", i)])]
                    units.append((i, segs, 16 * i, 16))
            prev = None
            for (u, segs, qc0, nq) in units[ulo:uhi]:
                p0 = attn_S(t, l, u, 0, segs, qc0, nq)
                yield
                if prev is not None:
                    attn_PV(t, l, *prev)
                    yield
                p1 = attn_S(t, l, u, 1, segs, qc0, nq)
                yield
                prev = (u, segs, p0, p1, qc0, nq)
            attn_PV(t, l, *prev)
            yield

        def carry_phase(t, l):
            if t.kind == "p" and not t.last:
                nb = len(t.blocks)
                nt = t.ncol // 128
                dve_copy(KTH[:, l, :], t.KT[:, t.ncol:t.ncol + 128], [(t.kp + "KT", nb - 1)], [("KTH", l)])
                dve_copy(VAH[:, l, :], t.VA[:, nt - 1, :], [(t.kp + "VT", nt - 1)], [("VH", l)])
                dve_copy(VBH[:, l, :], t.VB[:, nt - 1, :], [(t.kp + "VT", nt - 1)], [("VH", l)])

        def wout_block(t, l, Ws, bi):
            c0, n, _ = t.blocks[bi]
            for h in range(2):
                W, wk = Ws[h]
                for oc in range(4):
                    if bi >= 1 and h == 0 and oc == 2:
                        norm_B(t, l, 1, bi - 1)
                    ps, pk = main_ps.next()
                    pe_group(ps[:, 0:n], [(W[:, k, oc * 128:(oc + 1) * 128], t.R1[:, k, c0:c0 + n]) for k in range(KC)],
                             [wk] + r1k(ALLK, t, bi), [pk])
                    evac_gated(t, l, 0, h * 4 + oc, bi, ps, pk)
                    yield
            norm_A(t, l, 1, bi)

        def run(g):
            for _ in g:
                pass

        def chain(*gs):
            for g in gs:
                yield from g

        def interleave(ga, gb, ra=1, rb=1):
            da = db = False
            while not (da and db):
                for _ in range(ra):
                    if not da:
                        try:
                            next(ga)
                        except StopIteration:
                            da = True
                for _ in range(rb):
                    if not db:
                        try:
                            next(gb)
                        except StopIteration:
                            db = True

        def mixer_phase(t, l, caches, pending, co=None, cc=None):
            nb = len(t.blocks)
            upb = t.nunits // nb
            WW = win_begin(t, l, pending)

            def co_win(PW):
                if co is not None:
                    yield from win_block(co, l, cc, [], WW, 0)
                    win_end(co, l, WW)
                    yield
                    yield from pool_block(co, l, PW, 0)

            def co_attn():
                if co is not None:
                    yield from attn_gen(co, l, cc, 0, co.nunits)

            if nb == 1:
                run(win_block(t, l, caches, pending, WW, 0))
                win_end(t, l, WW)
                PW = wq.get(("pw", l))
                run(pool_block(t, l, PW, 0))
                run(co_win(PW))
                for _ in range(4):
                    wq.release()
                run(attn_gen(t, l, caches, 0, t.nunits))
                run(co_attn())
                carry_phase(t, l)
                Ws = [wq.get(("out", l, h)) for h in range(2)]
                run(wout_block(t, l, Ws, 0))
            else:
                assert nb == 2
                run(win_block(t, l, caches, pending, WW, 0))
                PW = wq.get(("pw", l))
                def tails():
                    win_end(t, l, WW)
                    yield
                interleave(chain(win_block(t, l, caches, pending, WW, 1), tails(), pool_block(t, l, PW, 0), co_win(PW)),
                           attn_gen(t, l, caches, 0, upb), 1, 2)
                for _ in range(3):
                    wq.release()
                carry_phase(t, l)
                Ws = [wq.get(("out", l, h)) for h in range(2)]
                interleave(chain(pool_block(t, l, PW, 1), wout_block(t, l, Ws, 0)),
                           chain(attn_gen(t, l, caches, upb, 2 * upb), co_attn()), 1, 3)
                wq.release()
                run(wout_block(t, l, Ws, 1))
            if co is not None:
                run(wout_block(co, l, Ws, 0))
                norm_B(co, l, 1, 0)
            wq.release()
            wq.release()
            return [lambda: norm_B(t, l, 1, nb - 1)]

        def ffn_phase(t, l, ti, pending, last_layer, nxt=None, co=None):
            nb = len(t.blocks)
            nsteps = 0
            work = [(t, bi) for bi in range(nb)] + ([(co, 0)] if co is not None else [])
            if nb == 1:
                while pending:
                    pending.pop(0)()
            for gi, (f0, f1) in enumerate(FGROUPS):
                ng = f1 - f0
                npieces = ng // 2
                for p0 in range(0, npieces, 2):
                    pis = list(range(p0, min(p0 + 2, npieces)))
                    Ws = [wq.get(("gu", l, gi, pi)) for pi in pis]
                    for (tw, bi) in work:
                        c0, n, _ = tw.blocks[bi]
                        for pi, (W, wk) in zip(pis, Ws):
                            for fi in range(2):
                                fl = 2 * pi + fi
                                psa, pka = ffn_ps.next()
                                psb, pkb = ffn_ps.next()
                                pe_group(psa[:, 0:n],
                                         [(W[:, k, fi * 128:(fi + 1) * 128], tw.R2[:, k, c0:c0 + n]) for k in range(KC)],
                                         [wk] + r2k(tw, ALLK, bi), [pka])
                                pe_group(psb[:, 0:n],
                                         [(W[:, k, 256 + fi * 128:256 + (fi + 1) * 128], tw.R2[:, k, c0:c0 + n])
                                          for k in range(KC)],
                                         [wk] + r2k(tw, ALLK, bi), [pkb])
                                sl, slk = tb.next()
                                act(sl[:, 0:n], psa[:, 0:n], AF.Silu, [pka], [slk])
                                dve_tt(tw.R1[:, fl, c0:c0 + n], sl[:, 0:n], psb[:, 0:n], ALU.mult, [slk, pkb],
                                       r1k([fl], tw, bi))
                                nsteps += 1
                                if nsteps == 1:
                                    while pending:
                                        pending.pop(0)()
                    for _ in pis:
                        wq.release()
                Ws = [wq.get(("dn", l, gi, h)) for h in range(2)]
                lastg = gi == len(FGROUPS) - 1
                for (tw, bi) in work:
                    c0, n, _ = tw.blocks[bi]
                    for h in range(2):
                        W, wk = Ws[h]
                        for oc in range(4):
                            ps, pk = ffn_ps.next()
                            pe_group(ps[:, 0:n],
                                     [(W[:, k, oc * 128:(oc + 1) * 128], tw.R1[:, k, c0:c0 + n]) for k in range(ng)],
                                     [wk] + r1k(range(ng), tw, bi), [pk])
                            if lastg and tw is t and not last_layer and bi >= 1 and h == 0 and oc == 2:
                                norm_B(t, l + 1, 0, bi - 1)
                            evac_gated(tw, l, 1, h * 4 + oc, bi, ps, pk)
                    if lastg and tw is t:
                        if last_layer:
                            if nxt is not None and bi == nb - 1:
                                nnt = nxt.ncol // 128 if nxt.kind == "p" else 1
                                for tt in range(min(2, nnt)):
                                    ld_dma(nxt, tt)
                            final_block(t, bi)
                        else:
                            norm_A(t, l + 1, 0, bi)
                    elif lastg:
                        if last_layer:
                            final_block(co, 0)
                        else:
                            norm_A(co, l + 1, 0, 0)
                            norm_B(co, l + 1, 0, 0)
                wq.release()
                wq.release()
            if last_layer:
                return []
            return [lambda: norm_B(t, l + 1, 0, nb - 1)]

        def load_caches(t, l):
            caches = []
            for i in range(4):
                cr = cache_ring.next()
                ktc, vc, cpt, kk, vk, ck_ = cr
                stg, sk, sch = cst.next()
                sp_dma(sch, stg[:, :], ck[l, i, :, :], (), [sk])
                ps, pk = main_ps.next()
                pe_T(ps[:, 0:128], stg[:, :], [sk], [pk])
                dve_copy(ktc[:, :], ps[:, 0:128], [pk], [kk])
                stg, sk, sch = cst.next()
                sp_dma(sch, stg[:, :], cv[l, i, :, :], (), [sk])
                dve_copy(vc[0][:, 0:64], stg[:, 0:64], [sk], [vk])
                dve_copy(vc[1][:, 64:128], stg[:, 64:128], [sk], [vk])
                stg, sk, sch = cps.next()
                sp_dma(sch, stg[0:15, :], cpool[l, i, :, :], (), [sk])
                ps, pk = main_ps.next()
                for g in range(4):
                    pe_T(ps[:, g * 16 + 1:g * 16 + 16], stg[0:15, g * 128:(g + 1) * 128], [sk], [pk])
                dve_copy(cpt[:, :, 1:16], ps[:, 0:64].rearrange("p (g c) -> p g c", g=4)[:, :, 1:16], [pk], [ck_])
                caches.append(cr)
            return caches

        def final_block(t, bi):
            c0, n, _ = t.blocks[bi]
            ntt = n // 128 if t.kind == "p" else 1
            rows = 128 if t.kind == "p" else 64
            for tq in range(ntt):
                col = c0 + tq * 128
                ring = all_ps if bi == len(t.blocks) - 1 else main_ps
                ps0, pk0 = ring.next()
                ps1, pk1 = ring.next()
                for kc in range(KC):
                    ps = ps0 if kc < 4 else ps1
                    pk = pk0 if kc < 4 else pk1
                    q4 = kc % 4
                    pe_T(ps[0:rows, q4 * 128:(q4 + 1) * 128], t.X[:, kc, col:col + rows], xk(t, [kc], bi), [pk])
                yo, yk, ych = yent = xin_free.pop(0)
                xin_free.append(yent)
                act(yo[0:rows, 0:512], ps0[0:rows, :], AF.Copy, [pk0], [yk])
                dve_copy(yo[0:rows, 512:1024], ps1[0:rows, :], [pk1], [yk])
                sq0, sk0 = tb.next()
                sq1, sk1 = tb.next()
                S.op("act", lambda e, rows=rows, sq0=sq0, yo=yo: e.activation(
                    out=sq0[0:rows, :], in_=yo[0:rows, 0:512], func=AF.Square, accum_out=SS[0:rows, 0:1]),
                    [yk], [sk0, "SS"])
                S.op("act", lambda e, rows=rows, sq1=sq1, yo=yo: e.activation(
                    out=sq1[0:rows, :], in_=yo[0:rows, 512:1024], func=AF.Square, accum_out=SS[0:rows, 1:2]),
                    [yk], [sk1, "SS"])
                dve_tt(SS[0:rows, 2:3], SS[0:rows, 0:1], SS[0:rows, 1:2], ALU.add, ["SS"], ["SS"])
                act(SS[0:rows, 3:4], SS[0:rows, 2:3], AF.Ln, ["SS", "EPSC"], ["SS"], scale=1.0 / D, bias=EPSC[0:rows, 0:1])
                act(SS[0:rows, 3:4], SS[0:rows, 3:4], AF.Exp, ["SS"], ["SS"], scale=-0.5)
                dve_stt(yo[0:rows, :], yo[0:rows, :], SS[0:rows, 3:4], GF[0:rows, :], ALU.mult, ALU.mult,
                        [yk, "SS", "GF"], [yk])
                if t.kind == "p":
                    sp_dma(ych, yp[t.seq, t.t0 + col:t.t0 + col + 128, :], yo[:, :], [yk], [])
                else:
                    sp_dma(ych, ys[:, :], yo[0:64, :], [yk], [])

        for t in tiles:
            t.ld = {}
        for ti, t in enumerate(main_tiles):
            if ti == 0:
                wq.consume_ada()
            pending = prologue(t, len(t.ld))
            cot = co_tile if ti == len(main_tiles) - 1 else None
            if cot is not None:
                for f in prologue(cot, 0):
                    f()
            nxt = main_tiles[ti + 1] if ti + 1 < len(main_tiles) else None
            for l in range(n_layers):
                caches = load_caches(t, l) if t.kind == "s" else None
                cc = load_caches(cot, l) if cot is not None else None
                pending = mixer_phase(t, l, caches, pending, cot, cc)
                last = l + 1 == n_layers
                pending = ffn_phase(t, l, ti, pending, last, nxt if last else None, cot)
        S.wait_all("sp")
        sems = {k: E(nc.semaphore(k)) for k in S.semkeys}
        S.emit(sems)
    return nc


N_CORES = 8
CFG = dict(n_seq=4, n_layers=4, TT=1024, do_sample=True, NS=4)


def make_in_maps(inputs, n_cores=N_CORES):
    f = lambda a: np.ascontiguousarray(np.asarray(a, dtype=np.float32))
    x_prompt, x_sample = f(inputs["x_prompt"]), f(inputs["x_sample"])
    cache_pool, cache_k, cache_v = f(inputs["cache_pool"]), f(inputs["cache_k"]), f(inputs["cache_v"])
    c_prompt, c_sample = f(inputs["c_prompt"]), f(inputs["c_sample"])
    shared = {
        "w_ada": f(inputs["w_ada"]),
        "b_ada": f(inputs["b_ada"]).reshape(NL, 48, 128),
        "g_mix": f(inputs["g_mix"]).reshape(32, 128),
        "w_in": f(inputs["w_in"]),
        "pool_w": f(inputs["pool_w"]),
        "pool_scale": f(inputs["pool_scale"]).reshape(16, 128),
        "sinks": f(inputs["sinks"]),
        "w_out": f(inputs["w_out"]),
        "g_ffn": f(inputs["g_ffn"]).reshape(32, 128),
        "w_gu": f(inputs["w_gate_up"]),
        "w_dn": f(inputs["w_down"]),
        "g_final": f(inputs["g_final"]).reshape(8, 128),
    }
    maps = []
    for i in range(n_cores):
        sl = slice(4 * i, 4 * i + 4)
        m = dict(shared)
        m["xp"] = np.ascontiguousarray(x_prompt[sl])
        m["xs"] = np.ascontiguousarray(x_sample[sl]).reshape(64, D)
        m["cpool"] = np.ascontiguousarray(cache_pool[:, sl])
        m["ck"] = np.ascontiguousarray(cache_k[:, sl]).reshape(NL, 4, 128, 128)
        m["cv"] = np.ascontiguousarray(cache_v[:, sl]).reshape(NL, 4, 128, 128)
        m["call"] = np.ascontiguousarray(np.concatenate([c_prompt[sl], c_sample[sl]], axis=0)).reshape(64, 128)
        maps.append(m)
    return maps


def gather_outputs(results):
    yp = np.concatenate([r["yp"] for r in results], axis=0)
    ys = np.concatenate([r["ys"].reshape(4, 16, D) for r in results], axis=0)
    poolp = np.concatenate([r["poolp"] for r in results], axis=1)
    kp = np.concatenate([r["kp"].reshape(NL, 4, 128, 2, 64) for r in results], axis=1)
    vp = np.concatenate([r["vp"].reshape(NL, 4, 128, 2, 64) for r in results], axis=1)
    pools = np.concatenate([r["pools"] for r in results], axis=1)
    ks = np.concatenate([r["ks"].reshape(NL, 4, 128, 2, 64) for r in results], axis=1)
    vs = np.concatenate([r["vs"].reshape(NL, 4, 128, 2, 64) for r in results], axis=1)
    return tuple(np.ascontiguousarray(a, dtype=np.float32) for a in (yp, ys, poolp, kp, vp, pools, ks, vs))


def kernel(**inputs):
    nc = build_program(**CFG)
    in_maps = make_in_maps(inputs)
    res = run_bass_kernel_spmd(nc, in_maps, core_ids=list(range(N_CORES)))
    return gather_outputs(res.results)
```

```python
import contextlib
import numpy as np
import concourse.bass as bass
import concourse.mybir as mybir
from concourse.bass_utils import run_bass_kernel_spmd

F32 = mybir.dt.float32
BF16 = mybir.dt.bfloat16
AF = mybir.ActivationFunctionType
ALU = mybir.AluOpType

D = 1024
KC = 8
NL = 4
SEQ = 2048
NF = 22
POOLW = (2, 4, 8, 16)
EPS = 1e-6
FGROUPS = ((0, 8), (8, 16), (16, 22))
ENGS = ("pe", "act", "dve", "pool", "sp")


class Sched:
    def __init__(self, nc):
        self.nc = nc
        self.streams = {e: [] for e in ENGS}
        self.count = {e: 0 for e in ENGS}
        self.seen = {e: {} for e in ENGS}
        self.last_w = {}
        self.readers = {}
        self.semkeys = list(ENGS)
        self.nchan = 0

    def new_chan(self):
        k = f"d{self.nchan}"
        self.nchan += 1
        self.semkeys.append(k)
        self.count[k] = 0
        return k

    def _need(self, eng, marker):
        if marker is None:
            return
        semkey, val, src = marker
        if src == eng and eng == "pe":
            return
        if self.seen[eng].get(semkey, 0) >= val:
            return
        self.seen[eng][semkey] = val
        self.streams[eng].append(("w", semkey, val))

    def _deps(self, eng, reads, writes):
        for r in reads:
            self._need(eng, self.last_w.get(r))
        for w in writes:
            self._need(eng, self.last_w.get(w))
            rd = self.readers.get(w)
            if rd:
                for sk, (v, src) in rd.items():
                    self._need(eng, (sk, v, src))

    def _commit(self, marker, reads, writes):
        semkey, val, src = marker
        for r in reads:
            d = self.readers.setdefault(r, {})
            if d.get(semkey, (0, None))[0] < val:
                d[semkey] = (val, src)
        for w in writes:
            self.last_w[w] = marker
            self.readers[w] = {}

    def op(self, eng, fn, reads=(), writes=()):
        self._deps(eng, reads, writes)
        self.count[eng] += 1
        marker = (eng, self.count[eng], eng)
        self.streams[eng].append(("i", fn, eng, 1))
        self._commit(marker, reads, writes)

    def dma(self, queue, chan, fn, reads=(), writes=()):
        self._deps(queue, reads, writes)
        self.count[chan] += 16
        marker = (chan, self.count[chan], "dma")
        self.streams[queue].append(("i", fn, chan, 16))
        self._commit(marker, reads, writes)

    def wait_all(self, eng):
        for k in self.semkeys:
            if self.count[k] > 0 and k != eng:
                self._need(eng, (k, self.count[k], "x"))

    def emit(self, sems):
        nc = self.nc
        streams = self.streams

        def run(engname, handle):
            for ent in streams[engname]:
                if ent[0] == "w":
                    handle.wait_ge(sems[ent[1]], ent[2])
                else:
                    ins = ent[1](handle)
                    ins.then_inc(sems[ent[2]], ent[3])

        with nc.Block() as block:
            @block.tensor
            def _(e):
                run("pe", e)

            @block.scalar
            def _(e):
                run("act", e)

            @block.vector
            def _(e):
                run("dve", e)

            @block.gpsimd
            def _(e):
                run("pool", e)

            @block.sync
            def _(e):
                run("sp", e)


class Ring:
    def __init__(self, items):
        self.items = items
        self.i = 0

    def next(self):
        it = self.items[self.i % len(self.items)]
        self.i += 1
        return it


class TileDesc:
    pass


def make_tiles(n_seq, TT, do_sample):
    tiles = []
    for s in range(n_seq):
        for t0 in range(0, SEQ, TT):
            t = TileDesc()
            t.kind = "p"
            t.seq = s
            t.t0 = t0
            t.ncol = TT
            t.blocks = [(b * 512, 512, [(b * 512, 512, s)]) for b in range(TT // 512)]
            t.first = t0 == 0
            t.last = t0 + TT == SEQ
            t.c0 = t0 // 64
            t.nunits = TT // 64
            t.units_of_block = lambda b: list(range(8 * b, 8 * b + 8))
            tiles.append(t)
    if do_sample:
        t = TileDesc()
        t.kind = "s"
        t.seq = None
        t.t0 = 0
        t.ncol = 64
        t.blocks = [(0, 64, [(16 * i, 16, 4 + i) for i in range(4)])]
        t.first = True
        t.last = True
        t.c0 = 0
        t.nunits = 4
        t.units_of_block = lambda b: [0, 1, 2, 3]
        tiles.append(t)
    return tiles


def piece_plan(tiles, n_layers):
    plan = []
    for ti, t in enumerate(tiles):
        for l in range(n_layers):
            if ti == 0 and l == 0:
                for i in range(12):
                    plan.append(("ada", 0, i))
            reg = [("u", l), ("q", l), ("kv", l), ("pw", l), ("out", l, 0), ("out", l, 1)]
            for gi, (f0, f1) in enumerate(FGROUPS):
                for pi in range((f1 - f0) // 2):
                    reg.append(("gu", l, gi, pi))
                reg += [("dn", l, gi, 0), ("dn", l, gi, 1)]
            if ti == 0 and l + 1 < n_layers:
                out, na = [], 0
                for k, d in enumerate(reg):
                    out.append(d)
                    if k >= 6 and na < 12:
                        out.append(("ada", l + 1, na))
                        na += 1
                reg = out
            plan += reg
    return plan


def build_program(n_seq=4, n_layers=4, TT=1024, do_sample=True, NS=4, dbg_stop=None):
    nc = bass.Bass("TRN2", target_bir_lowering=False)
    S = Sched(nc)

    def din(name, shape):
        return nc.dram_tensor(name, shape, F32, kind="ExternalInput").ap()

    def dout(name, shape):
        return nc.dram_tensor(name, shape, F32, kind="ExternalOutput").ap()

    xp = din("xp", [4, SEQ, D])
    xs = din("xs", [64, D])
    cpool = din("cpool", [NL, 4, 15, 512])
    ck = din("ck", [NL, 4, 128, 128])
    cv = din("cv", [NL, 4, 128, 128])
    call = din("call", [64, 128])
    w_ada = din("w_ada", [NL, D, 6 * D])
    b_ada = din("b_ada", [NL, 48, 128])
    g_mix = din("g_mix", [32, 128])
    w_in = din("w_in", [NL, D, 1280])
    pool_w = din("pool_w", [NL, 4, 128, 128])
    pool_scale = din("pool_scale", [16, 128])
    sinks = din("sinks", [NL, 8])
    w_out = din("w_out", [NL, D, D])
    g_ffn = din("g_ffn", [32, 128])
    w_gu = din("w_gu", [NL, D, 2 * NF * 128])
    w_dn = din("w_dn", [NL, NF * 128, D])
    g_final = din("g_final", [8, 128])

    yp = dout("yp", [4, SEQ, D])
    ys = dout("ys", [64, D])
    poolp = dout("poolp", [NL, 4, 15, 512])
    kp = dout("kp", [NL, 4, 128, 128])
    vp = dout("vp", [NL, 4, 128, 128])
    pools = dout("pools", [NL, 4, 15, 512])
    ks = dout("ks", [NL, 4, 128, 128])
    vs = dout("vs", [NL, 4, 128, 128])

    tiles = make_tiles(n_seq, TT, do_sample)
    main_tiles = [t_ for t_ in tiles if t_.kind == "p"]
    co_tile = None
    if main_tiles and len(main_tiles) < len(tiles):
        co_tile = [t_ for t_ in tiles if t_.kind == "s"][0]
    else:
        main_tiles = tiles
    plan = piece_plan(main_tiles, n_layers)
    reg_ids = {}
    for d in plan:
        if d[0] != "ada" and d not in reg_ids:
            reg_ids[d] = len(reg_ids)
    scr = nc.dram_tensor("wscr", [max(len(reg_ids), 1), 128, 8 * 512], BF16, kind="Internal").ap()
    first_use = {}
    for i, d in enumerate(plan):
        first_use.setdefault(d, i)
    scr_chan = []
    NCH = TT // 64

    with contextlib.ExitStack() as st:
        E = st.enter_context

        def sb(name, shape, dt=F32):
            return E(nc.sbuf_tensor(name, shape, dt))

        X = sb("X", [128, KC, TT])
        R1 = sb("R1", [128, KC, TT], BF16)
        R2 = sb("R2", [128, KC, TT], BF16)
        KT = sb("KT", [128, 128 + TT], BF16)
        NT = max(TT // 128, 4)
        VA = sb("VA", [128, NT, 128], BF16)
        VB = sb("VB", [128, NT, 128], BF16)
        Q2 = sb("Q2", [128, 4, TT], BF16)
        Xs = sb("Xs", [128, KC, 64])
        R1s = sb("R1s", [128, KC, 64], BF16)
        R2s = sb("R2s", [128, KC, 64], BF16)
        KTs = sb("KTs", [128, 320], BF16)
        VAs = sb("VAs", [128, 4, 128], BF16)
        VBs = sb("VBs", [128, 4, 128], BF16)
        Q2s = sb("Q2s", [128, 4, 64], BF16)
        SQs = sb("SQs", [128, KC, 64], BF16)
        WS = [sb(f"WS{i}", [128, 8, 512], BF16) for i in range(NS)]
        SQ = [sb(f"SQ{i}", [128, KC, 512], BF16) for i in range(1)]
        TB = [sb(f"TB{i}", [128, 512]) for i in range(5)]
        RSB = [sb(f"RSB{i}", [128, 512]) for i in range(2)]
        UB = [sb(f"UB{i}", [128, 528]) for i in range(6)]
        XIN = [sb(f"XIN{i}", [128, D]) for i in range(4)]
        GF = sb("GF", [128, D])
        PTF = [sb(f"PTF{i}", [128, 256], BF16) for i in range(5)]
        PTL = [sb(f"PTL{i}", [128, 256], BF16) for i in range(4)]
        PTH = [sb(f"PTH{i}", [128, 256], BF16) for i in range(4)]
        PTS = [sb(f"PTS{i}", [128, 256], BF16) for i in range(4)]
        OA = sb("OA", [128, 128], BF16)
        OB = sb("OB", [128, 128], BF16)
        KVO = [sb(f"KVO{i}", [128, 256]) for i in range(2)]
        POUT = [sb(f"POUT{i}", [16, 512]) for i in range(1)]
        KTC = [sb(f"KTC{i}", [128, 128], BF16) for i in range(4)]
        VCA = [sb(f"VCA{i}", [128, 128], BF16) for i in range(4)]
        VCB = [sb(f"VCB{i}", [128, 128], BF16) for i in range(4)]
        CST = [sb(f"CST{i}", [128, 128]) for i in range(1)]
        CPS = [sb(f"CPS{i}", [16, 512]) for i in range(1)]
        CPT = [sb(f"CPT{i}", [128, 4, 16]) for i in range(4)]
        MODC = sb("MODC", [128, NL, 6, KC, 8])
        GSC = sb("GSC", [128, NL, 2, 8, KC])
        IDENT = sb("IDENT", [128, 128])
        ONES = sb("ONES", [128, 128], BF16)
        ONESF = sb("ONESF", [128, 128])
        BADA = sb("BADA", [128, NL, 48])
        GM = sb("GM", [128, 2, 32])
        PSC = sb("PSC", [128, 16])
        GFC = sb("GFC", [128, 8])
        SCT = sb("SCT", [128, 8, 8], BF16)
        SKX = sb("SKX", [128, NL, 4])
        UH = sb("UH", [128, NL, 4, 16])
        KTH = sb("KTH", [128, NL, 128], BF16)
        VAH = sb("VAH", [128, NL, 128], BF16)
        VBH = sb("VBH", [128, NL, 128], BF16)
        INVC = sb("INVC", [128, 4, 16])
        EPSC = sb("EPSC", [128, 1])
        STG = sb("STG", [128, 128])
        SS = sb("SS", [128, 4])

        for t_ in tiles:
            if t_.kind == "p":
                t_.X, t_.R1, t_.R2, t_.KT, t_.VA, t_.VB, t_.Q2, t_.kp = X, R1, R2, KT, VA, VB, Q2, "m"
                t_.SQ = SQ[0]
            else:
                t_.X, t_.R1, t_.R2, t_.KT, t_.VA, t_.VB, t_.Q2, t_.kp = Xs, R1s, R2s, KTs, VAs, VBs, Q2s, "s"
                t_.SQ = SQs
        PB = [E(nc.psum_tensor(f"PB{i}", [128, 512], F32)) for i in range(8)]
        main_ps = Ring([(PB[i], ("PS", i)) for i in range(3)])
        s_ps = Ring([(PB[3 + i], ("PS", 3 + i)) for i in range(3)])
        od_ps = Ring([(PB[6 + i], ("PS", 6 + i)) for i in range(2)])
        ffn_ps = Ring([(PB[i], ("PS", i)) for i in range(3, 8)])
        all_ps = Ring([(PB[i], ("PS", i)) for i in range(8)])

        tb = Ring([(TB[i], ("TB", i)) for i in range(5)])
        rsb = Ring([(RSB[i], ("RSB", i)) for i in range(2)])
        ub = Ring([(UB[i], ("UB", i)) for i in range(6)])
        xin_free = [(XIN[i], ("XIN", i), S.new_chan()) for i in range(4)]
        pt = {"full": Ring([(PTF[i], ("PTF", i)) for i in range(5)]),
              "lo": Ring([(PTL[i], ("PTL", i)) for i in range(4)]),
              "hi": Ring([(PTH[i], ("PTH", i)) for i in range(4)]),
              "s16": Ring([(PTS[i], ("PTS", i)) for i in range(4)])}
        MROWS = {"full": (0, 128), "lo": (0, 64), "hi": (64, 128), "s16": (0, 16)}
        kvo = Ring([(KVO[i], ("KVO", i), S.new_chan()) for i in range(2)])
        pout = Ring([(POUT[i], ("POUT", i), S.new_chan()) for i in range(1)])
        cst = Ring([(CST[i], ("CST", i), S.new_chan()) for i in range(1)])
        cps = Ring([(CPS[i], ("CPS", i), S.new_chan()) for i in range(1)])
        cache_ring = Ring([(KTC[i], (VCA[i], VCB[i]), CPT[i], ("KTC", i), ("VC", i), ("CPT", i)) for i in range(4)])
        ws_chan = [S.new_chan() for _ in range(NS)]
        d2d_chan = S.new_chan()

        def pe_group(out, pairs, reads, writes):
            def fn(e):
                n = len(pairs)
                ins = None
                for i, (l, r) in enumerate(pairs):
                    ins = e.matmul(out, lhsT=l, rhs=r, start=(i == 0), stop=(i == n - 1))
                return ins
            S.op("pe", fn, reads, writes)

        def pe_T(out, in_, reads, writes):
            S.op("pe", lambda e: e.transpose(out, in_, IDENT[0:in_.shape[0], 0:in_.shape[0]]),
                 list(reads) + ["IDENT"], writes)

        def act(out, in_, func, reads, writes, **kw):
            S.op("act", lambda e: e.activation(out=out, in_=in_, func=func, **kw), reads, writes)

        def dve_tt(out, in0, in1, op, reads, writes):
            S.op("dve", lambda e: e.tensor_tensor(out=out, in0=in0, in1=in1, op=op), reads, writes)

        def dve_stt(out, in0, scalar, in1, op0, op1, reads, writes):
            S.op("dve", lambda e: e.scalar_tensor_tensor(out=out, in0=in0, scalar=scalar, in1=in1,
                                                        op0=op0, op1=op1), reads, writes)

        def dve_ts(out, in0, s1, s2, op0, op1, reads, writes):
            S.op("dve", lambda e: e.tensor_scalar(out=out, in0=in0, scalar1=s1, scalar2=s2, op0=op0, op1=op1),
                 reads, writes)

        def dve_copy(out, in_, reads, writes):
            S.op("dve", lambda e: e.tensor_copy(out=out, in_=in_), reads, writes)

        def dve_recip(out, in_, reads, writes):
            S.op("dve", lambda e: e.reciprocal(out=out, in_=in_), reads, writes)

        def dve_memset(out, val, writes):
            S.op("dve", lambda e: e.memset(out, val), (), writes)

        def sp_dma(chan, out, in_, reads, writes):
            S.dma("sp", chan, lambda e: e.dma_start(out=out, in_=in_), reads, writes)

        class WStream:
            def __init__(self):
                self.next_issue = 0
                self.next_use = 0
                self.free = list(range(NS))
                self.slot_of = {}
                self.held = []

            def _issue(self, i):
                desc = plan[i]
                slot = self.free.pop(0)
                self.slot_of[i] = slot
                W = WS[slot]
                key = ("W", slot)
                ch = ws_chan[slot]
                kind, l = desc[0], desc[1]
                if kind != "ada" and first_use[desc] != i:
                    rid = reg_ids[desc]
                    S.dma("pool", ch, lambda e: e.dma_start(out=W[:, :, :].rearrange("p k c -> p (k c)"),
                                                            in_=scr[rid]), [("SCR", rid)], [key])
                    return

                def dm(out, in_):
                    S.dma("pool", ch, lambda e: e.dma_start(out=out, in_=in_), (), [key])

                if kind == "ada":
                    i0 = desc[2] * 512
                    dm(W[:, :, :], w_ada[l].rearrange("(k p) c -> p k c", p=128)[:, :, i0:i0 + 512])
                elif kind == "u":
                    dm(W[:, :, :], w_in[l].rearrange("(k p) c -> p k c", p=128)[:, :, 0:512])
                elif kind == "q":
                    src = w_in[l].rearrange("(k p) c -> p k c", p=128)
                    for b in range(4):
                        for half in range(2):
                            h = half * 4 + b
                            dm(W[:, :, b * 128 + half * 64: b * 128 + half * 64 + 64],
                               src[:, :, 512 + 64 * h: 512 + 64 * h + 64])
                elif kind == "kv":
                    dm(W[:, :, 0:256], w_in[l].rearrange("(k p) c -> p k c", p=128)[:, :, 1024:1280])
                elif kind == "pw":
                    dm(W[:, 0:4, 0:128], pool_w[l].rearrange("g c d -> c g d"))
                elif kind == "out":
                    c0 = desc[2] * 512
                    dm(W[:, 0:4, :], w_out[l, 0:512, :].rearrange("(k p) c -> p k c", p=128)[:, :, c0:c0 + 512])
                    dm(W[0:64, 4:8, :], w_out[l, 512:768, :].rearrange("(b p) c -> p b c", p=64)[:, :, c0:c0 + 512])
                    dm(W[64:128, 4:8, :], w_out[l, 768:1024, :].rearrange("(b p) c -> p b c", p=64)[:, :, c0:c0 + 512])
                elif kind == "gu":
                    gi, pi = desc[2], desc[3]
                    f = FGROUPS[gi][0] + 2 * pi
                    src = w_gu[l].rearrange("(k p) c -> p k c", p=128)
                    dm(W[:, :, 0:256], src[:, :, 128 * f: 128 * f + 256])
                    dm(W[:, :, 256:512], src[:, :, NF * 128 + 128 * f: NF * 128 + 128 * f + 256])
                elif kind == "dn":
                    gi, h = desc[2], desc[3]
                    f0, f1 = FGROUPS[gi]
                    dm(W[:, 0:f1 - f0, :],
                       w_dn[l, 128 * f0:128 * f1, :].rearrange("(k p) c -> p k c", p=128)[:, :, h * 512:h * 512 + 512])
                else:
                    raise AssertionError(kind)
                if kind != "ada" and len(main_tiles) > 1:
                    rid = reg_ids[desc]
                    if not scr_chan:
                        scr_chan.extend(S.new_chan() for _ in range(NS))
                    S.dma("sp", scr_chan[slot], lambda e: e.dma_start(out=scr[rid],
                                                                       in_=W[:, :, :].rearrange("p k c -> p (k c)")),
                          [key], [("SCR", rid)])

            def _pump(self):
                while self.next_issue < len(plan) and self.free:
                    self._issue(self.next_issue)
                    self.next_issue += 1

            def consume_ada(self):
                self._pump()
                while self.next_use < len(plan) and plan[self.next_use][0] == "ada":
                    i = self.next_use
                    self.next_use += 1
                    assert i < self.next_issue
                    sl = self.slot_of[i]
                    ada_step(plan[i][1], plan[i][2], WS[sl], ("W", sl))
                    self._mark(i)

            def get(self, desc):
                self.consume_ada()
                i = self.next_use
                assert plan[i] == desc, (plan[i], desc)
                assert i < self.next_issue, (i, self.next_issue, desc)
                self.next_use += 1
                self.held.append(i)
                sl = self.slot_of[i]
                return WS[sl], ("W", sl)

            def _mark(self, i):
                self.free.append(self.slot_of.pop(i))
                self._pump()

            def release(self):
                self._mark(self.held.pop(0))

        wq = WStream()

        S.op("pool", lambda e: e.memset(ONESF[:], 1.0), (), ["ONESF"])
        S.op("pool", lambda e: e.affine_select(out=IDENT[:], in_=ONESF[:], pattern=[[1, 128]],
                                               compare_op=ALU.is_equal, fill=0.0, base=0,
                                               channel_multiplier=-1), ["ONESF"], ["IDENT"])
        dve_memset(ONES[:], 1.0, ["ONES"])
        dve_memset(OA[:], 0.0, ["OA"])
        dve_memset(OA[:, 0:64], 1.0, ["OA"])
        dve_memset(OB[:], 0.0, ["OB"])
        dve_memset(OB[:, 64:128], 1.0, ["OB"])
        dve_memset(VA[:], 0.0, [("mVT", i) for i in range(NT)])
        dve_memset(VB[:], 0.0, [("mVT", i) for i in range(NT)])
        dve_memset(Q2[:], 0.0, [("mQ2", b_) for b_ in range(max(TT // 512, 1))])
        dve_memset(VAs[:], 0.0, [("sVT", i) for i in range(4)])
        dve_memset(VBs[:], 0.0, [("sVT", i) for i in range(4)])
        dve_memset(Q2s[:], 0.0, [("sQ2", 0)])
        dve_memset(KTs[:], 0.0, [("sKT", 0)])
        for i in range(4):
            dve_memset(VCA[i][:], 0.0, [("VC", i)])
            dve_memset(VCB[i][:], 0.0, [("VC", i)])
            dve_memset(PTL[i][:], 0.0, [("PTL", i)])
            dve_memset(PTH[i][:], 0.0, [("PTH", i)])
            dve_memset(PTS[i][:], 0.0, [("PTS", i)])
        dve_memset(EPSC[:], EPS, ["EPSC"])
        for g, w in enumerate(POOLW):
            dve_memset(INVC[:, g, :], 1.0 / w, ["INVC"])
            for t in range(w - 1):
                dve_memset(INVC[:, g, t:t + 1], 1.0 / (t + 1), ["INVC"])

        def load_T(src_rows_ap, nrows, dst_ap, dst_keys, func=None):
            ch = S.new_chan()
            sp_dma(ch, STG[0:nrows, :], src_rows_ap, (), ["STG"])
            ps, pk = main_ps.next()
            pe_T(ps[:, 0:nrows], STG[0:nrows, :], ["STG"], [pk])
            if func is None:
                dve_copy(dst_ap, ps[:, 0:nrows], [pk], dst_keys)
            else:
                act(dst_ap, ps[:, 0:nrows], func, [pk], dst_keys)

        for l in range(n_layers):
            load_T(b_ada[l], 48, BADA[:, l, :], ["BADA"])
        load_T(g_mix, 32, GM[:, 0, :], ["GM"])
        load_T(g_ffn, 32, GM[:, 1, :], ["GM"])
        load_T(pool_scale, 16, PSC[:, :], ["PSC"])
        load_T(g_final, 8, GFC[:, :], ["GFC"])
        load_T(call, 64, SCT[:].rearrange("p s k -> p (s k)"), ["SCT"], func=AF.Silu)
        ch = S.new_chan()
        sp_dma(ch, SKX[0:64, :, :], sinks[:, 0:4].unsqueeze(0).to_broadcast([64, NL, 4]), (), ["SKX"])
        ch = S.new_chan()
        sp_dma(ch, SKX[64:128, :, :], sinks[:, 4:8].unsqueeze(0).to_broadcast([64, NL, 4]), (), ["SKX"])
        act(SKX[:], SKX[:], AF.Exp, ["SKX"], ["SKX"])
        ch = S.new_chan()
        sp_dma(ch, GF[:], g_final.rearrange("a b -> (a b)").partition_broadcast(128), (), ["GF"])

        def r1k(kcs, t, blk):
            ks_ = []
            for kc in kcs:
                if kc < 4:
                    ks_.append((t.kp + "R1", kc, "b", blk))
                else:
                    ks_ += [(t.kp + "R1", kc, u) for u in t.units_of_block(blk)]
            return ks_

        def r2k(t, kcs, blk):
            return [(t.kp + "R2", kc, blk) for kc in kcs]

        def xk(t, kcs, blk):
            return [(t.kp + "X", kc, blk) for kc in kcs]

        ALLK = list(range(KC))

        def ada_step(l, i, W, wk):
            ps, pk = main_ps.next()
            for c4 in range(4):
                pe_group(ps[:, c4 * 8:c4 * 8 + 8],
                         [(W[:, k, c4 * 128:(c4 + 1) * 128], SCT[:, :, k]) for k in range(KC)],
                         [wk, "SCT"], [pk])
            for c4 in range(4):
                fc = i * 4 + c4
                j, kc = fc // 8, fc % 8
                act(MODC[:, l, j, kc, :], ps[:, c4 * 8:c4 * 8 + 8], AF.Identity, [pk, "BADA"], [("MODC", l)],
                    bias=BADA[:, l, fc:fc + 1])
            if i == 11:
                for w in range(2):
                    for kc in range(KC):
                        dve_ts(GSC[:, l, w, :, kc], MODC[:, l, 1 + 3 * w, kc, :], 1.0,
                               GM[:, w, l * 8 + kc:l * 8 + kc + 1],
                               ALU.add, ALU.mult, [("MODC", l), "GM"], [("GSC", l)])

        def ld_dma(t, tt):
            buf, bk, ch = ent = xin_free.pop(0)
            if t.kind == "p":
                sp_dma(ch, buf[:, :], xp[t.seq, t.t0 + tt * 128: t.t0 + tt * 128 + 128, :], (), [bk])
            else:
                sp_dma(ch, buf[0:64, :], xs[:, :], (), [bk])
            t.ld[tt] = ent

        def ld_tr(t, tt):
            ent = t.ld.pop(tt)
            buf, bk, _ = ent
            xin_free.append(ent)
            rows = 128 if t.kind == "p" else 64
            blk = (tt * 128) // 512
            for hf in range(2):
                ps, pk = all_ps.next()
                for q4 in range(4):
                    kc = hf * 4 + q4
                    pe_T(ps[:, q4 * 128: q4 * 128 + rows], buf[0:rows, kc * 128:(kc + 1) * 128], [bk], [pk])
                dve_copy(t.X[:, hf * 4:hf * 4 + 4, tt * 128: tt * 128 + rows],
                         ps[:, :].rearrange("p (q c) -> p q c", q=4)[:, :, 0:rows],
                         [pk], xk(t, range(hf * 4, hf * 4 + 4), blk))

        def prologue(t, prefetched):
            if t.kind == "p" and t.first:
                dve_memset(UH[:], 0.0, [("UH", l_) for l_ in range(NL)])
            ntt = t.ncol // 128 if t.kind == "p" else 1
            nb = len(t.blocks)
            issued = prefetched
            done = 0
            for bi in range(nb):
                hi = ntt if bi == nb - 1 else (bi + 1) * 4
                while done < hi:
                    while issued < min(ntt, done + 3) and xin_free:
                        ld_dma(t, issued)
                        issued += 1
                    ld_tr(t, done)
                    done += 1
                norm_A(t, 0, 0, bi)
                if bi < nb - 1:
                    norm_B(t, 0, 0, bi)
            return [lambda: norm_B(t, 0, 0, nb - 1)]

        def norm_A(t, l, w, bi):
            c0, n, subs = t.blocks[bi]
            sq = t.SQ
            act(sq[:, 0:4, 0:n], t.X[:, 0:4, c0:c0 + n], AF.Square, xk(t, range(4), bi), [(t.kp + "SQ", 0)])
            dve_tt(sq[:, 4:8, 0:n], t.X[:, 4:8, c0:c0 + n], t.X[:, 4:8, c0:c0 + n], ALU.mult, xk(t, range(4, 8), bi), [(t.kp + "SQ", 1)])

        def norm_B(t, l, w, bi):
            dst = t.R1 if w == 0 else t.R2
            dstk = (lambda kcs: r1k(kcs, t, bi)) if w == 0 else (lambda kcs: r2k(t, kcs, bi))
            c0, n, subs = t.blocks[bi]
            sq = t.SQ
            ps, pk = main_ps.next()
            pe_group(ps[:, 0:n], [(ONES[:, :], sq[:, k, 0:n]) for k in range(KC)], [(t.kp + "SQ", 0), (t.kp + "SQ", 1), "ONES"], [pk])
            rs, rk = rsb.next()
            act(rs[:, 0:n], ps[:, 0:n], AF.Ln, [pk, "EPSC"], [rk], scale=1.0 / D, bias=EPSC[:, 0:1])
            act(rs[:, 0:n], rs[:, 0:n], AF.Exp, [rk], [rk], scale=-0.5)
            for (sc0, sn, si) in subs:
                o = sc0 - c0
                for kc in range(KC):
                    tmp, tk = tb.next()
                    dve_stt(tmp[:, 0:sn], t.X[:, kc, sc0:sc0 + sn], GSC[:, l, w, si, kc:kc + 1], rs[:, o:o + sn],
                            ALU.mult, ALU.mult, xk(t, [kc], bi) + [rk, ("GSC", l)], [tk])
                    act(dst[:, kc, sc0:sc0 + sn], tmp[:, 0:sn], AF.Identity, [tk, ("MODC", l)], dstk([kc]),
                        bias=MODC[:, l, 3 * w, kc, si:si + 1])

        def evac_gated(t, l, w, kc, bi, ps, pk):
            c0, n, subs = t.blocks[bi]
            for (sc0, sn, si) in subs:
                o = sc0 - c0
                dve_stt(t.X[:, kc, sc0:sc0 + sn], ps[:, o:o + sn], MODC[:, l, 2 + 3 * w, kc, si:si + 1],
                        t.X[:, kc, sc0:sc0 + sn], ALU.mult, ALU.add, [pk, ("MODC", l)] + xk(t, [kc], bi), xk(t, [kc], bi))

        def vt_block(t, l, W, wk, bi):
            H = t.R1
            ps, pk = main_ps.next()
            for q in range(4):
                ti_ = 4 * bi + q
                pe_group(ps[:, q * 128:(q + 1) * 128],
                         [(H[:, k, 128 * ti_:128 * ti_ + 128], W[:, k, 128:256]) for k in range(KC)],
                         [wk] + r1k(ALLK, t, bi), [pk])
            pv = ps[:, :].rearrange("p (q c) -> p q c", q=4)
            vkeys = [(t.kp + "VT", 4 * bi + q) for q in range(4)]
            dve_copy(t.VA[:, 4 * bi:4 * bi + 4, 0:64], pv[:, :, 0:64], [pk], vkeys)
            dve_copy(t.VB[:, 4 * bi:4 * bi + 4, 64:128], pv[:, :, 64:128], [pk], vkeys)

        def win_begin(t, l, pending):
            if len(t.blocks) == 1:
                while pending:
                    pending.pop(0)()
            return (wq.get(("u", l)), wq.get(("q", l)), wq.get(("kv", l)))

        def win_block(t, l, caches, pending, WW, bi):
            H = t.R1
            hk = lambda bi: r1k(ALLK, t, bi)
            (Wu, wuk), (Wq, wqk), (Wkv, wkk) = WW
            bc0, bn, _ = t.blocks[bi]
            if True:
                if t.kind == "p":
                    units = [(bc0, bn, None)]
                else:
                    units = [(16 * i, 16, i) for i in range(4)]
                for g, wdw in enumerate(POOLW):
                    for (c0, n, si) in units:
                        ps, pk = main_ps.next()
                        pe_group(ps[:, 0:n], [(Wu[:, k, g * 128:(g + 1) * 128], H[:, k, c0:c0 + n]) for k in range(KC)],
                                 [wuk] + hk(bi), [pk])
                        U, uk = ub.next()
                        act(U[:, 16:16 + n], ps[:, 0:n], AF.Copy, [pk], [uk])
                        if t.kind == "p":
                            dve_copy(U[:, 1:16], UH[:, l, g, 1:16], [("UH", l)], [uk])
                        else:
                            dve_copy(U[:, 1:16], caches[si][2][:, g, 1:16], [caches[si][5]], [uk])
                        prev, prevk = U, uk
                        lo = 1
                        for lv in range(g + 1):
                            sh = 1 << lv
                            lo2 = lo + sh
                            Sn, sk = ub.next()
                            dve_tt(Sn[:, lo2:16 + n], prev[:, lo2:16 + n], prev[:, lo2 - sh:16 + n - sh], ALU.add,
                                   [prevk], [sk])
                            prev, prevk, lo = Sn, sk, lo2
                        dkey = r2k(t, [4 + g], bi)
                        dve_stt(t.R2[:, 4 + g, c0:c0 + n], prev[:, 16:16 + n], 1.0 / wdw, U[:, 16:16 + n],
                                ALU.mult, ALU.subtract, [prevk, uk], dkey)
                        if t.kind == "p" and t.first and bi == 0:
                            tmp, tk = tb.next()
                            dve_tt(tmp[:, 0:16], prev[:, 16:32], INVC[:, g, :], ALU.mult, [prevk, "INVC"], [tk])
                            dve_tt(t.R2[:, 4 + g, c0:c0 + 16], tmp[:, 0:16], U[:, 16:32], ALU.subtract, [tk, uk], dkey)
                        if t.kind == "p":
                            dve_copy(UH[:, l, g, 1:16], U[:, n + 1:n + 16], [uk], [("UH", l)])
                        yield
                act(t.R2[64:128, 0:4, bc0:bc0 + bn], t.X[64:128, 0:4, bc0:bc0 + bn], AF.Copy, xk(t, range(4), bi),
                    r2k(t, range(4), bi), scale=0.0)
                for oc in range(4):
                    ps, pk = main_ps.next()
                    pe_group(ps[:, 0:bn], [(Wq[:, k, oc * 128:(oc + 1) * 128], H[:, k, bc0:bc0 + bn]) for k in range(KC)],
                             [wqk] + hk(bi), [pk])
                    act(t.R2[0:64, oc, bc0:bc0 + bn], ps[0:64, 0:bn], AF.Copy, [pk], r2k(t, [oc], bi))
                    act(t.Q2[64:128, oc, bc0:bc0 + bn], ps[64:128, 0:bn], AF.Copy, [pk], [(t.kp + "Q2", bi)])
                    yield
                ps, pk = main_ps.next()
                pe_group(ps[:, 0:bn], [(Wkv[:, k, 0:128], H[:, k, bc0:bc0 + bn]) for k in range(KC)], [wkk] + hk(bi), [pk])
                act(t.KT[:, 128 + bc0:128 + bc0 + bn], ps[:, 0:bn], AF.Copy, [pk], [(t.kp + "KT", bi)])
                yield
                while pending:
                    pending.pop(0)()
                if t.kind == "p":
                    vt_block(t, l, Wkv, wkk, bi)
                yield

        def win_end(t, l, WW):
            H = t.R1
            hk = lambda bi: r1k(ALLK, t, bi)
            (Wu, wuk), (Wq, wqk), (Wkv, wkk) = WW
            nb = len(t.blocks)
            if t.kind == "p" and t.last:
                ps, pk = main_ps.next()
                pe_group(ps[0:16, :], [(H[:, k, t.ncol - 16:t.ncol], Wu[:, k, :]) for k in range(KC)],
                         [wuk] + hk(nb - 1), [pk])
                po, pok, pch = pout.next()
                dve_copy(po[0:16, :], ps[0:16, :], [pk], [pok])
                sp_dma(pch, poolp[l, t.seq, :, :], po[1:16, :], [pok], [])
                ps, pk = main_ps.next()
                pe_group(ps[:, 0:256], [(H[:, k, t.ncol - 128:t.ncol], Wkv[:, k, 0:256]) for k in range(KC)],
                         [wkk] + hk(nb - 1), [pk])
                ko, kok, kch = kvo.next()
                dve_copy(ko[:, :], ps[:, 0:256], [pk], [kok])
                sp_dma(kch, kp[l, t.seq, :, :], ko[:, 0:128], [kok], [])
                sp_dma(kch, vp[l, t.seq, :, :], ko[:, 128:256], [kok], [])
            if t.kind == "s":
                for i in range(4):
                    ps, pk = main_ps.next()
                    pe_group(ps[0:16, :], [(H[:, k, 16 * i:16 * i + 16], Wu[:, k, :]) for k in range(KC)],
                             [wuk] + hk(0), [pk])
                    po, pok, pch = pout.next()
                    dve_copy(po[0:16, :], ps[0:16, :], [pk], [pok])
                    sp_dma(pch, pools[l, i, :, :], po[1:16, :], [pok], [])
                    ps, pk = main_ps.next()
                    pe_group(ps[0:16, 0:256], [(H[:, k, 16 * i:16 * i + 16], Wkv[:, k, 0:256]) for k in range(KC)],
                             [wkk] + hk(0), [pk])
                    ko, kok, kch = kvo.next()
                    dve_copy(ko[0:16, :], ps[0:16, 0:256], [pk], [kok])
                    dve_copy(t.VA[0:16, i, 0:64], ps[0:16, 128:192], [pk], [(t.kp + "VT", i)])
                    dve_copy(t.VB[0:16, i, 64:128], ps[0:16, 192:256], [pk], [(t.kp + "VT", i)])
                    sp_dma(kch, ks[l, i, 112:128, :], ko[0:16, 0:128], [kok], [])
                    sp_dma(kch, vs[l, i, 112:128, :], ko[0:16, 128:256], [kok], [])
                    sp_dma(d2d_chan, ks[l, i, 0:112, :], ck[l, i, 16:128, :], (), [])
                    sp_dma(d2d_chan, vs[l, i, 0:112, :], cv[l, i, 16:128, :], (), [])

        def pool_block(t, l, PW, bi):
            W, wk = PW
            c0, n, _ = t.blocks[bi]
            for g in range(4):
                ps, pk = main_ps.next()
                pe_group(ps[:, 0:n], [(W[:, g, 0:128], t.R2[:, 4 + g, c0:c0 + n])], [wk] + r2k(t, [4 + g], bi), [pk])
                act(t.R1[:, g, c0:c0 + n], ps[:, 0:n], AF.Identity, [pk, "PSC"], r1k([g], t, bi),
                    scale=PSC[:, l * 4 + g:l * 4 + g + 1])
                yield

        def attn_S(t, l, u, j, segs, qc0, nq):
            N = 4 * nq
            qblk = qc0 // 512
            Qj = t.R2 if j == 0 else t.Q2
            qkeys = r2k(t, range(4), qblk) if j == 0 else [(t.kp + "Q2", qblk)]
            pts = []
            for si_, (kT, va, vb, mask, rkeys) in enumerate(segs):
                if si_ % 2 == 0:
                    pss, sk = s_ps.next()
                hf = si_ % 2
                pe_group(pss[:, hf * 256:hf * 256 + N], [(kT, Qj[:, 0:4, qc0:qc0 + nq])], list(rkeys) + qkeys, [sk])
                pts.append([None, None, pss, hf, sk, mask])
            for ent in pts:
                pss, hf, sk, mask = ent[2], ent[3], ent[4], ent[5]
                p, pk_ = pt[mask].next()
                r0, r1 = MROWS[mask]
                act(p[r0:r1, 0:N], pss[r0:r1, hf * 256:hf * 256 + N], AF.Exp, [sk], [pk_], scale=0.125)
                ent[0], ent[1] = p, pk_
            return [(e[0], e[1]) for e in pts]

        def attn_PV(t, l, u, segs, pts0, pts1, qc0, nq):
            N = 4 * nq
            ob, obk = od_ps.next()
            pairs_o, pairs_d, rk = [], [], []
            for (seg, (p0, k0), (p1, k1)) in zip(segs, pts0, pts1):
                kT, va, vb, mask, rkeys = seg
                pairs_o += [(va, p0[:, 0:N]), (vb, p1[:, 0:N])]
                pairs_d += [(OA[:, :], p0[:, 0:N]), (OB[:, :], p1[:, 0:N])]
                rk += [k0, k1] + list(rkeys)
            pe_group(ob[:, 0:N], pairs_o, rk, [obk])
            pe_group(ob[:, 256:256 + N], pairs_d, rk + ["OA", "OB"], [obk])
            den, dkk = tb.next()
            dve_tt(den[:, 0:N].rearrange("p (b q) -> p b q", b=4),
                   ob[:, 256:256 + N].rearrange("p (b q) -> p b q", b=4),
                   SKX[:, l, :].unsqueeze(2).to_broadcast([128, 4, nq]), ALU.add,
                   [obk, "SKX"], [dkk])
            act(den[:, 0:N], den[:, 0:N], AF.Ln, [dkk], [dkk])
            act(den[:, 0:N], den[:, 0:N], AF.Exp, [dkk], [dkk], scale=-1.0)
            dve_tt(t.R1[:, 4:8, qc0:qc0 + nq], ob[:, 0:N].rearrange("p (b q) -> p b q", b=4),
                   den[:, 0:N].rearrange("p (b q) -> p b q", b=4), ALU.mult,
                   [obk, dkk], [(t.kp + "R1", 4 + b, u) for b in range(4)])

        def attn_gen(t, l, caches, ulo, uhi):
            units = []

            def tile_seg(idx, mask):
                return (t.KT[:, 128 + 128 * idx:128 + 128 * idx + 128], t.VA[:, idx, :], t.VB[:, idx, :], mask,
                        [(t.kp + "KT", (128 * idx) // 512), (t.kp + "VT", idx)])

            def hist_seg(mask):
                return (KTH[:, l, :], VAH[:, l, :], VBH[:, l, :], mask, [("KTH", l), ("VH", l)])

            if t.kind == "p":
                for c in range(t.nunits):
                    segs = []
                    if c % 2 == 0:
                        if c >= 2:
                            segs.append(tile_seg(c // 2 - 1, "full"))
                        elif not t.first:
                            segs.append(hist_seg("full"))
                        segs.append(tile_seg(c // 2, "lo"))
                    else:
                        if c >= 3:
                            segs.append(tile_seg((c - 3) // 2, "hi"))
                        elif not t.first:
                            segs.append(hist_seg("hi"))
                        segs.append(tile_seg((c - 1) // 2, "full"))
                    units.append((c, segs, 64 * c, 64))
            else:
                for i in range(4):
                    ktc, (vca, vcb), _, kk, vk, _ = caches[i]
                    segs = [(ktc[:, :], vca[:, :], vcb[:, :], "full", [kk, vk]),
                            (t.KT[:, 128 + 16 * i:128 + 16 * i + 128], t.VA[:, i, :], t.VB[:, i, :], "s16",
                             [(t.kp + "KT", 0), (t.kp + "VT", i)])]
                    units.append((i, segs, 16 * i, 16))
            prev = None
            for (u, segs, qc0, nq) in units[ulo:uhi]:
                p0 = attn_S(t, l, u, 0, segs, qc0, nq)
                yield
                p1 = attn_S(t, l, u, 1, segs, qc0, nq)
                yield
                if prev is not None:
                    attn_PV(t, l, *prev)
                    yield
                prev = (u, segs, p0, p1, qc0, nq)
            attn_PV(t, l, *prev)
            yield

        def carry_phase(t, l):
            if t.kind == "p" and not t.last:
                nb = len(t.blocks)
                nt = t.ncol // 128
                dve_copy(KTH[:, l, :], t.KT[:, t.ncol:t.ncol + 128], [(t.kp + "KT", nb - 1)], [("KTH", l)])
                dve_copy(VAH[:, l, :], t.VA[:, nt - 1, :], [(t.kp + "VT", nt - 1)], [("VH", l)])
                dve_copy(VBH[:, l, :], t.VB[:, nt - 1, :], [(t.kp + "VT", nt - 1)], [("VH", l)])

        def wout_block(t, l, Ws, bi):
            c0, n, _ = t.blocks[bi]
            for h in range(2):
                W, wk = Ws[h]
                for oc in range(4):
                    if bi >= 1 and h == 0 and oc == 2:
                        norm_B(t, l, 1, bi - 1)
                    ps, pk = main_ps.next()
                    pe_group(ps[:, 0:n], [(W[:, k, oc * 128:(oc + 1) * 128], t.R1[:, k, c0:c0 + n]) for k in range(KC)],
                             [wk] + r1k(ALLK, t, bi), [pk])
                    evac_gated(t, l, 0, h * 4 + oc, bi, ps, pk)
                    yield
            norm_A(t, l, 1, bi)

        def run(g):
            for _ in g:
                pass

        def chain(*gs):
            for g in gs:
                yield from g

        def interleave(ga, gb, ra=1, rb=1):
            da = db = False
            while not (da and db):
                for _ in range(ra):
                    if not da:
                        try:
                            next(ga)
                        except StopIteration:
                            da = True
                for _ in range(rb):
                    if not db:
                        try:
                            next(gb)
                        except StopIteration:
                            db = True

        def mixer_phase(t, l, caches, pending, co=None, cc=None):
            nb = len(t.blocks)
            upb = t.nunits // nb
            WW = win_begin(t, l, pending)

            def co_win(PW):
                if co is not None:
                    yield from win_block(co, l, cc, [], WW, 0)
                    win_end(co, l, WW)
                    yield
                    yield from pool_block(co, l, PW, 0)

            def co_attn():
                if co is not None:
                    yield from attn_gen(co, l, cc, 0, co.nunits)

            if nb == 1:
                run(win_block(t, l, caches, pending, WW, 0))
                win_end(t, l, WW)
                PW = wq.get(("pw", l))
                run(pool_block(t, l, PW, 0))
                run(co_win(PW))
                for _ in range(4):
                    wq.release()
                run(attn_gen(t, l, caches, 0, t.nunits))
                run(co_attn())
                carry_phase(t, l)
                Ws = [wq.get(("out", l, h)) for h in range(2)]
                run(wout_block(t, l, Ws, 0))
            else:
                assert nb == 2
                run(win_block(t, l, caches, pending, WW, 0))
                PW = wq.get(("pw", l))
                run(pool_block(t, l, PW, 0))
                def tails():
                    win_end(t, l, WW)
                    yield
                interleave(chain(win_block(t, l, caches, pending, WW, 1), tails(), pool_block(t, l, PW, 1), co_win(PW)),
                           attn_gen(t, l, caches, 0, upb), 1, 2)
                for _ in range(4):
                    wq.release()
                carry_phase(t, l)
                Ws = [wq.get(("out", l, h)) for h in range(2)]
                interleave(wout_block(t, l, Ws, 0), chain(attn_gen(t, l, caches, upb, 2 * upb), co_attn()), 1, 3)
                run(wout_block(t, l, Ws, 1))
            if co is not None:
                run(wout_block(co, l, Ws, 0))
                norm_B(co, l, 1, 0)
            wq.release()
            wq.release()
            return [lambda: norm_B(t, l, 1, nb - 1)]

        def ffn_phase(t, l, ti, pending, last_layer, nxt=None, co=None):
            nb = len(t.blocks)
            nsteps = 0
            work = [(t, bi) for bi in range(nb)] + ([(co, 0)] if co is not None else [])
            if nb == 1:
                while pending:
                    pending.pop(0)()
            for gi, (f0, f1) in enumerate(FGROUPS):
                ng = f1 - f0
                npieces = ng // 2
                for p0 in range(0, npieces, 2):
                    pis = list(range(p0, min(p0 + 2, npieces)))
                    Ws = [wq.get(("gu", l, gi, pi)) for pi in pis]
                    for (tw, bi) in work:
                        c0, n, _ = tw.blocks[bi]
                        for pi, (W, wk) in zip(pis, Ws):
                            for fi in range(2):
                                fl = 2 * pi + fi
                                psa, pka = ffn_ps.next()
                                psb, pkb = ffn_ps.next()
                                pe_group(psa[:, 0:n],
                                         [(W[:, k, fi * 128:(fi + 1) * 128], tw.R2[:, k, c0:c0 + n]) for k in range(KC)],
                                         [wk] + r2k(tw, ALLK, bi), [pka])
                                pe_group(psb[:, 0:n],
                                         [(W[:, k, 256 + fi * 128:256 + (fi + 1) * 128], tw.R2[:, k, c0:c0 + n])
                                          for k in range(KC)],
                                         [wk] + r2k(tw, ALLK, bi), [pkb])
                                sl, slk = tb.next()
                                act(sl[:, 0:n], psa[:, 0:n], AF.Silu, [pka], [slk])
                                dve_tt(tw.R1[:, fl, c0:c0 + n], sl[:, 0:n], psb[:, 0:n], ALU.mult, [slk, pkb],
                                       r1k([fl], tw, bi))
                                nsteps += 1
                                if nsteps == 1:
                                    while pending:
                                        pending.pop(0)()
                    for _ in pis:
                        wq.release()
                Ws = [wq.get(("dn", l, gi, h)) for h in range(2)]
                lastg = gi == len(FGROUPS) - 1
                for (tw, bi) in work:
                    c0, n, _ = tw.blocks[bi]
                    for h in range(2):
                        W, wk = Ws[h]
                        for oc in range(4):
                            ps, pk = ffn_ps.next()
                            pe_group(ps[:, 0:n],
                                     [(W[:, k, oc * 128:(oc + 1) * 128], tw.R1[:, k, c0:c0 + n]) for k in range(ng)],
                                     [wk] + r1k(range(ng), tw, bi), [pk])
                            if lastg and tw is t and not last_layer and bi >= 1 and h == 0 and oc == 2:
                                norm_B(t, l + 1, 0, bi - 1)
                            evac_gated(tw, l, 1, h * 4 + oc, bi, ps, pk)
                    if lastg and tw is t:
                        if last_layer:
                            if nxt is not None and bi == nb - 1:
                                nnt = nxt.ncol // 128 if nxt.kind == "p" else 1
                                for tt in range(min(2, nnt)):
                                    ld_dma(nxt, tt)
                            final_block(t, bi)
                        else:
                            norm_A(t, l + 1, 0, bi)
                    elif lastg:
                        if last_layer:
                            final_block(co, 0)
                        else:
                            norm_A(co, l + 1, 0, 0)
                            norm_B(co, l + 1, 0, 0)
                wq.release()
                wq.release()
            if last_layer:
                return []
            return [lambda: norm_B(t, l + 1, 0, nb - 1)]

        def load_caches(t, l):
            caches = []
            for i in range(4):
                cr = cache_ring.next()
                ktc, vc, cpt, kk, vk, ck_ = cr
                stg, sk, sch = cst.next()
                sp_dma(sch, stg[:, :], ck[l, i, :, :], (), [sk])
                ps, pk = main_ps.next()
                pe_T(ps[:, 0:128], stg[:, :], [sk], [pk])
                dve_copy(ktc[:, :], ps[:, 0:128], [pk], [kk])
                stg, sk, sch = cst.next()
                sp_dma(sch, stg[:, :], cv[l, i, :, :], (), [sk])
                dve_copy(vc[0][:, 0:64], stg[:, 0:64], [sk], [vk])
                dve_copy(vc[1][:, 64:128], stg[:, 64:128], [sk], [vk])
                stg, sk, sch = cps.next()
                sp_dma(sch, stg[0:15, :], cpool[l, i, :, :], (), [sk])
                ps, pk = main_ps.next()
                for g in range(4):
                    pe_T(ps[:, g * 16 + 1:g * 16 + 16], stg[0:15, g * 128:(g + 1) * 128], [sk], [pk])
                dve_copy(cpt[:, :, 1:16], ps[:, 0:64].rearrange("p (g c) -> p g c", g=4)[:, :, 1:16], [pk], [ck_])
                caches.append(cr)
            return caches

        def final_block(t, bi):
            c0, n, _ = t.blocks[bi]
            ntt = n // 128 if t.kind == "p" else 1
            rows = 128 if t.kind == "p" else 64
            for tq in range(ntt):
                col = c0 + tq * 128
                ring = all_ps if bi == len(t.blocks) - 1 else main_ps
                ps0, pk0 = ring.next()
                ps1, pk1 = ring.next()
                for kc in range(KC):
                    ps = ps0 if kc < 4 else ps1
                    pk = pk0 if kc < 4 else pk1
                    q4 = kc % 4
                    pe_T(ps[0:rows, q4 * 128:(q4 + 1) * 128], t.X[:, kc, col:col + rows], xk(t, [kc], bi), [pk])
                yo, yk, ych = yent = xin_free.pop(0)
                xin_free.append(yent)
                act(yo[0:rows, 0:512], ps0[0:rows, :], AF.Copy, [pk0], [yk])
                dve_copy(yo[0:rows, 512:1024], ps1[0:rows, :], [pk1], [yk])
                sq0, sk0 = tb.next()
                sq1, sk1 = tb.next()
                S.op("act", lambda e, rows=rows, sq0=sq0, yo=yo: e.activation(
                    out=sq0[0:rows, :], in_=yo[0:rows, 0:512], func=AF.Square, accum_out=SS[0:rows, 0:1]),
                    [yk], [sk0, "SS"])
                S.op("act", lambda e, rows=rows, sq1=sq1, yo=yo: e.activation(
                    out=sq1[0:rows, :], in_=yo[0:rows, 512:1024], func=AF.Square, accum_out=SS[0:rows, 1:2]),
                    [yk], [sk1, "SS"])
                dve_tt(SS[0:rows, 2:3], SS[0:rows, 0:1], SS[0:rows, 1:2], ALU.add, ["SS"], ["SS"])
                act(SS[0:rows, 3:4], SS[0:rows, 2:3], AF.Ln, ["SS", "EPSC"], ["SS"], scale=1.0 / D, bias=EPSC[0:rows, 0:1])
                act(SS[0:rows, 3:4], SS[0:rows, 3:4], AF.Exp, ["SS"], ["SS"], scale=-0.5)
                dve_stt(yo[0:rows, :], yo[0:rows, :], SS[0:rows, 3:4], GF[0:rows, :], ALU.mult, ALU.mult,
                        [yk, "SS", "GF"], [yk])
                if t.kind == "p":
                    sp_dma(ych, yp[t.seq, t.t0 + col:t.t0 + col + 128, :], yo[:, :], [yk], [])
                else:
                    sp_dma(ych, ys[:, :], yo[0:64, :], [yk], [])

        for t in tiles:
            t.ld = {}
        for ti, t in enumerate(main_tiles):
            if ti == 0:
                wq.consume_ada()
            pending = prologue(t, len(t.ld))
            cot = co_tile if ti == len(main_tiles) - 1 else None
            if cot is not None:
                for f in prologue(cot, 0):
                    f()
            nxt = main_tiles[ti + 1] if ti + 1 < len(main_tiles) else None
            for l in range(n_layers):
                caches = load_caches(t, l) if t.kind == "s" else None
                cc = load_caches(cot, l) if cot is not None else None
                pending = mixer_phase(t, l, caches, pending, cot, cc)
                last = l + 1 == n_layers
                pending = ffn_phase(t, l, ti, pending, last, nxt if last else None, cot)
        S.wait_all("sp")
        sems = {k: E(nc.semaphore(k)) for k in S.semkeys}
        S.emit(sems)
    return nc


N_CORES = 8
CFG = dict(n_seq=4, n_layers=4, TT=1024, do_sample=True, NS=4)


def make_in_maps(inputs, n_cores=N_CORES):
    f = lambda a: np.ascontiguousarray(np.asarray(a, dtype=np.float32))
    x_prompt, x_sample = f(inputs["x_prompt"]), f(inputs["x_sample"])
    cache_pool, cache_k, cache_v = f(inputs["cache_pool"]), f(inputs["cache_k"]), f(inputs["cache_v"])
    c_prompt, c_sample = f(inputs["c_prompt"]), f(inputs["c_sample"])
    shared = {
        "w_ada": f(inputs["w_ada"]),
        "b_ada": f(inputs["b_ada"]).reshape(NL, 48, 128),
        "g_mix": f(inputs["g_mix"]).reshape(32, 128),
        "w_in": f(inputs["w_in"]),
        "pool_w": f(inputs["pool_w"]),
        "pool_scale": f(inputs["pool_scale"]).reshape(16, 128),
        "sinks": f(inputs["sinks"]),
        "w_out": f(inputs["w_out"]),
        "g_ffn": f(inputs["g_ffn"]).reshape(32, 128),
        "w_gu": f(inputs["w_gate_up"]),
        "w_dn": f(inputs["w_down"]),
        "g_final": f(inputs["g_final"]).reshape(8, 128),
    }
    maps = []
    for i in range(n_cores):
        sl = slice(4 * i, 4 * i + 4)
        m = dict(shared)
        m["xp"] = np.ascontiguousarray(x_prompt[sl])
        m["xs"] = np.ascontiguousarray(x_sample[sl]).reshape(64, D)
        m["cpool"] = np.ascontiguousarray(cache_pool[:, sl])
        m["ck"] = np.ascontiguousarray(cache_k[:, sl]).reshape(NL, 4, 128, 128)
        m["cv"] = np.ascontiguousarray(cache_v[:, sl]).reshape(NL, 4, 128, 128)
        m["call"] = np.ascontiguousarray(np.concatenate([c_prompt[sl], c_sample[sl]], axis=0)).reshape(64, 128)
        maps.append(m)
    return maps


def gather_outputs(results):
    yp = np.concatenate([r["yp"] for r in results], axis=0)
    ys = np.concatenate([r["ys"].reshape(4, 16, D) for r in results], axis=0)
    poolp = np.concatenate([r["poolp"] for r in results], axis=1)
    kp = np.concatenate([r["kp"].reshape(NL, 4, 128, 2, 64) for r in results], axis=1)
    vp = np.concatenate([r["vp"].reshape(NL, 4, 128, 2, 64) for r in results], axis=1)
    pools = np.concatenate([r["pools"] for r in results], axis=1)
    ks = np.concatenate([r["ks"].reshape(NL, 4, 128, 2, 64) for r in results], axis=1)
    vs = np.concatenate([r["vs"].reshape(NL, 4, 128, 2, 64) for r in results], axis=1)
    return tuple(np.ascontiguousarray(a, dtype=np.float32) for a in (yp, ys, poolp, kp, vp, pools, ks, vs))


def kernel(**inputs):
    nc = build_program(**CFG)
    in_maps = make_in_maps(inputs)
    res = run_bass_kernel_spmd(nc, in_maps, core_ids=list(range(N_CORES)))
    return gather_outputs(res.results)
```
